# Optimizing a Trainium2 kernel written in Bass

```python
import math
import jax, jax.numpy as jnp
from jax import lax
import numpy as np

D_MODEL = 1024
BATCH = 8
SEQ = 4096
DEPTH = 1

D_DN = 512
DN_HEADS = 4
DN_HEAD_DIM = 128
CONV_WIDTH = 4
CHUNK = 64
D_ATT = 512
ATT_HEADS = 8
ATT_HEAD_DIM = 64
DILATED_PATTERNS = ((128, 1), (512, 4), (2048, 16))
N_BUCKETS = 32
MAX_DISTANCE = 2048
D_MIX = D_DN + D_ATT
D_IN = 4 * D_DN + 2 * DN_HEADS + 4 * D_ATT
EPS = 1e-6

kernel_name = "hybrid_deltanet_dilated_attention_layer"


def rms_norm(x, w):
    xf = x.astype(jnp.float32)
    return xf * lax.rsqrt(jnp.mean(xf * xf, axis=-1, keepdims=True) + EPS) * w.astype(jnp.float32)


def l2_norm(x):
    return x * lax.rsqrt(jnp.sum(x * x, axis=-1, keepdims=True) + EPS)


def split_heads(t, n_heads):
    b, s, _ = t.shape
    return t.reshape(b, s, n_heads, -1).transpose(0, 2, 1, 3)


def causal_depthwise_conv(u, w):
    k_width = w.shape[0]
    s = u.shape[1]
    up = jnp.pad(u, ((0, 0), (k_width - 1, 0), (0, 0)))
    y = up[:, 0:s] * w[0]
    for i in range(1, k_width):
        y = y + up[:, i:i + s] * w[i]
    return y


def gated_delta_rule(q, k, v, g, beta):
    b, h, s, dk = q.shape
    dv = v.shape[-1]
    n = s // CHUNK
    q = q * (dk ** -0.5)
    qc = q.reshape(b, h, n, CHUNK, dk)
    kc = k.reshape(b, h, n, CHUNK, dk)
    vc = v.reshape(b, h, n, CHUNK, dv)
    bc = beta.reshape(b, h, n, CHUNK)
    gc = jnp.cumsum(g.reshape(b, h, n, CHUNK), axis=-1)
    tril_incl = np.tril(np.ones((CHUNK, CHUNK), dtype=bool))
    tril_strict = np.tril(np.ones((CHUNK, CHUNK), dtype=bool), -1)
    diff = gc[..., :, None] - gc[..., None, :]
    decay = jnp.exp(jnp.where(tril_incl, diff, -jnp.inf))
    kb = kc * bc[..., None]
    a_mat = jnp.where(tril_strict, jnp.einsum('bhnid,bhnjd->bhnij', kb, kc) * decay, 0.0)
    l_mat = a_mat + jnp.eye(CHUNK, dtype=a_mat.dtype)
    u = lax.linalg.triangular_solve(l_mat, vc * bc[..., None], left_side=True, lower=True, unit_diagonal=True)
    w = lax.linalg.triangular_solve(l_mat, kb * jnp.exp(gc)[..., None], left_side=True, lower=True, unit_diagonal=True)
    attn_intra = jnp.einsum('bhnid,bhnjd->bhnij', qc, kc) * decay
    q_dec = qc * jnp.exp(gc)[..., None]
    k_tail = kc * jnp.exp(gc[..., -1:] - gc)[..., None]
    g_last = jnp.exp(gc[..., -1])
    xs = tuple(jnp.moveaxis(t, 2, 0) for t in (attn_intra, q_dec, k_tail, u, w, g_last))

    def step(state, inp):
        a_i, qd_i, kt_i, u_i, w_i, gl_i = inp
        v_new = u_i - jnp.einsum('bhck,bhkv->bhcv', w_i, state)
        o_i = jnp.einsum('bhck,bhkv->bhcv', qd_i, state) + jnp.einsum('bhij,bhjv->bhiv', a_i, v_new)
        state = state * gl_i[..., None, None] + jnp.einsum('bhck,bhcv->bhkv', kt_i, v_new)
        return state, o_i

    s0 = jnp.zeros((b, h, dk, dv), jnp.float32)
    _, o = lax.scan(step, s0, xs)
    return jnp.moveaxis(o, 0, 2).reshape(b, h, s, dv)


def t5_bucket(dist):
    max_exact = N_BUCKETS // 2
    d = np.maximum(dist, 1).astype(np.float64)
    large = max_exact + (np.log(d / max_exact) / math.log(MAX_DISTANCE / max_exact)
                         * (N_BUCKETS - max_exact)).astype(np.int32)
    large = np.minimum(large, N_BUCKETS - 1)
    return np.where(dist < max_exact, dist, large).astype(np.int32)


def dilated_pattern(q, k, v, rel_bias, window, dilation):
    b, h, s, hd = q.shape
    r = dilation
    l_sub = s // r
    w_steps = window // r
    blk = w_steps
    n_blk = -(-l_sub // blk)
    l_pad = n_blk * blk

    def to_blocks(t):
        t = t.reshape(b, h, l_sub, r, hd).transpose(0, 1, 3, 2, 4)
        t = jnp.pad(t, ((0, 0), (0, 0), (0, 0), (0, l_pad - l_sub), (0, 0)))
        return t.reshape(b, h, r, n_blk, blk, hd)

    def with_prev(t):
        prev = jnp.pad(t, ((0, 0), (0, 0), (0, 0), (1, 0), (0, 0), (0, 0)))[:, :, :, :-1]
        return jnp.concatenate([prev, t], axis=4)

    qb = to_blocks(q)
    kw = with_prev(to_blocks(k))
    vw = with_prev(to_blocks(v))
    qi = np.arange(blk)[:, None]
    kj = np.arange(2 * blk)[None, :]
    step = qi - kj + blk
    band = (step >= 0) & (step <= w_steps)
    key_idx = np.arange(n_blk)[:, None, None] * blk + kj[None] - blk
    mask = band[None] & (key_idx >= 0)
    buckets = t5_bucket(np.clip(step, 0, None) * r)
    bias = rel_bias.astype(jnp.float32)[:, buckets]
    scores = jnp.einsum('bhrnqd,bhrnkd->bhrnqk', qb, kw).astype(jnp.float32) + bias[:, None, None]
    scores = jnp.where(mask, scores, -jnp.inf)
    lse = jax.nn.logsumexp(scores, axis=-1)
    p = jnp.exp(scores - lse[..., None])
    o = jnp.einsum('bhrnqk,bhrnkd->bhrnqd', p, vw.astype(jnp.float32))
    o = o.reshape(b, h, r, l_pad, hd)[:, :, :, :l_sub].transpose(0, 1, 3, 2, 4).reshape(b, h, s, hd)
    lse = lse.reshape(b, h, r, l_pad)[:, :, :, :l_sub].transpose(0, 1, 3, 2).reshape(b, h, s)
    return o, lse


def deltanet_branch(qkv, z, b_proj, a_proj, conv_w, a_log, dt_bias, dn_norm_w):
    bsz, s, _ = qkv.shape
    qkv = jax.nn.silu(causal_depthwise_conv(qkv.astype(jnp.float32), conv_w.astype(jnp.float32)))
    q, k, v = jnp.split(qkv, 3, axis=-1)
    q = l2_norm(split_heads(q, DN_HEADS))
    k = l2_norm(split_heads(k, DN_HEADS))
    v = split_heads(v, DN_HEADS)
    beta = jax.nn.sigmoid(b_proj.astype(jnp.float32)).transpose(0, 2, 1)
    g = -jnp.exp(a_log.astype(jnp.float32)) * jax.nn.softplus(a_proj.astype(jnp.float32) + dt_bias.astype(jnp.float32))
    g = g.transpose(0, 2, 1)
    o = gated_delta_rule(q, k, v, g, beta)
    o = rms_norm(o, dn_norm_w).transpose(0, 2, 1, 3).reshape(bsz, s, D_DN)
    return o * jax.nn.silu(z.astype(jnp.float32))


def dilated_attention_branch(qkv, gate, q_norm_w, k_norm_w, rel_bias):
    bsz, s, _ = qkv.shape
    q, k, v = jnp.split(qkv, 3, axis=-1)
    q = rms_norm(split_heads(q, ATT_HEADS), q_norm_w) * (ATT_HEAD_DIM ** -0.5)
    k = rms_norm(split_heads(k, ATT_HEADS), k_norm_w)
    v = split_heads(v, ATT_HEADS).astype(jnp.float32)
    outs, lses = [], []
    for window, dilation in DILATED_PATTERNS:
        o_p, lse_p = dilated_pattern(q, k, v, rel_bias, window, dilation)
        outs.append(o_p)
        lses.append(lse_p)
    wts = jax.nn.softmax(jnp.stack(lses), axis=0)
    o = jnp.sum(wts[..., None] * jnp.stack(outs), axis=0)
    o = o.transpose(0, 2, 1, 3).reshape(bsz, s, D_ATT)
    return o * jax.nn.silu(gate.astype(jnp.float32))


def hybrid_layer(x, norm_w, w_in, conv_w, a_log, dt_bias, dn_norm_w, q_norm_w, k_norm_w, rel_bias, w_out):
    h = rms_norm(x, norm_w).astype(x.dtype)
    proj = h @ w_in
    cuts = [3 * D_DN, 4 * D_DN, 4 * D_DN + DN_HEADS, 4 * D_DN + 2 * DN_HEADS,
            4 * D_DN + 2 * DN_HEADS + 3 * D_ATT]
    qkv_dn, z_dn, b_dn, a_dn, qkv_att, gate_att = jnp.split(proj, cuts, axis=-1)
    y_dn = deltanet_branch(qkv_dn, z_dn, b_dn, a_dn, conv_w, a_log, dt_bias, dn_norm_w)
    y_att = dilated_attention_branch(qkv_att, gate_att, q_norm_w, k_norm_w, rel_bias)
    mixed = jnp.concatenate([y_dn, y_att], axis=-1).astype(x.dtype)
    return x + mixed @ w_out


def setup_inputs(seed: int = 0) -> dict:
    key = jax.random.key(seed)
    ks = jax.random.split(key, 11)
    x = jax.random.normal(ks[0], (BATCH, SEQ, D_MODEL), jnp.float32)
    norm_w = 1.0 + 0.1 * jax.random.normal(ks[1], (DEPTH, D_MODEL), jnp.float32)
    w_in = jax.random.normal(ks[2], (DEPTH, D_MODEL, D_IN), jnp.float32) * (D_MODEL ** -0.5)
    conv_w = jax.random.normal(ks[3], (DEPTH, CONV_WIDTH, 3 * D_DN), jnp.float32) * (CONV_WIDTH ** -0.5)
    a_log = jnp.log(jax.random.uniform(ks[4], (DEPTH, DN_HEADS), jnp.float32, minval=1.0, maxval=16.0))
    dt = jnp.exp(jax.random.uniform(ks[5], (DEPTH, DN_HEADS), jnp.float32,
                                    minval=math.log(1e-3), maxval=math.log(1e-1)))
    dt_bias = dt + jnp.log(-jnp.expm1(-dt))
    dn_norm_w = 1.0 + 0.1 * jax.random.normal(ks[6], (DEPTH, DN_HEAD_DIM), jnp.float32)
    q_norm_w = 1.0 + 0.1 * jax.random.normal(ks[7], (DEPTH, ATT_HEAD_DIM), jnp.float32)
    k_norm_w = 1.0 + 0.1 * jax.random.normal(ks[8], (DEPTH, ATT_HEAD_DIM), jnp.float32)
    rel_bias = 0.5 * jax.random.normal(ks[9], (ATT_HEADS, N_BUCKETS), jnp.float32)
    w_out = jax.random.normal(ks[10], (DEPTH, D_MIX, D_MODEL), jnp.float32) * (D_MIX ** -0.5)
    return {"x": x, "norm_w": norm_w, "w_in": w_in, "conv_w": conv_w, "a_log": a_log,
            "dt_bias": dt_bias, "dn_norm_w": dn_norm_w, "q_norm_w": q_norm_w,
            "k_norm_w": k_norm_w, "rel_bias": rel_bias, "w_out": w_out}


def reference(x, norm_w, w_in, conv_w, a_log, dt_bias, dn_norm_w, q_norm_w, k_norm_w, rel_bias, w_out):
    for layer in range(DEPTH):
        x = hybrid_layer(x, norm_w[layer], w_in[layer], conv_w[layer], a_log[layer], dt_bias[layer],
                         dn_norm_w[layer], q_norm_w[layer], k_norm_w[layer], rel_bias, w_out[layer])
    return x
```

```python
import math
import os
from contextlib import ExitStack
import numpy as np
import ml_dtypes
import concourse.bass as bass
import concourse.mybir as mybir
from concourse.bass_utils import run_bass_kernel_spmd

F32 = mybir.dt.float32
BF16 = mybir.dt.bfloat16
AF = mybir.ActivationFunctionType
ALU = mybir.AluOpType

S = 4096
D = 1024
NT = 32
D_IN = 4104
EPS = 1e-6
NEG = -30000.0
PATTERNS = ((128, 1), (512, 4), (2048, 16))
N_BUCKETS = 32
MAX_DISTANCE = 2048


class Dep:
    __slots__ = ("w", "r", "name", "excl")

    def __init__(self, name="", excl=False):
        self.w = []
        self.r = []
        self.name = name
        self.excl = excl


class Tracker:
    def __init__(self, nc, es):
        self.nc = nc
        self.eng = {"pe": nc.tensor, "act": nc.scalar, "dve": nc.vector, "pool": nc.gpsimd, "sp": nc.sync}
        self.sem = {}
        self.cnt = {}
        self.known = {e: {} for e in self.eng}
        self.es = es
        for e in ("pe", "act", "dve", "pool"):
            self.sem[e] = es.enter_context(nc.semaphore("sem_" + e))
            self.cnt[e] = 0
        self.chan = {}
        self.semobj = {}
        self.pe_pending = []
        self.pe_last = None
        for e in ("pe", "act", "dve", "pool"):
            self.semobj[id(self.sem[e])] = self.sem[e]

    def _chan(self, name):
        if name not in self.chan:
            s = self.es.enter_context(self.nc.semaphore("ch_" + name))
            self.chan[name] = [s, 0]
        return self.chan[name]

    def _collect(self, eng, reads, writes):
        need = {}

        def add(tok, war):
            s, v, e = tok
            if e == eng and eng == "pe":
                return
            k = id(s)
            if k not in need or need[k][1] < v:
                need[k] = (s, v)

        for d in reads:
            for t in d.w:
                add(t, False)
            if d.excl:
                for t in d.r:
                    if t[2] != eng:
                        add(t, False)
        for d in writes:
            for t in d.w:
                add(t, False)
            for t in d.r:
                add(t, True)
        return need

    def _emit_waits(self, eng, need):
        kn = self.known[eng]
        for k, (s, v) in need.items():
            if kn.get(k, 0) < v:
                self.eng[eng].wait_ge(s, v)
                kn[k] = v

    def _record(self, tok, reads, writes, accumulate=False):
        for d in writes:
            if accumulate:
                d.w.append(tok)
            else:
                d.w = [tok]
                d.r = []
        for d in reads:
            d.r = [t for t in d.r if not (t[2] == tok[2] and t[2] != "dma")] + [tok]

    def _flush_pe(self):
        if self.pe_last is None:
            return
        self.cnt["pe"] += 1
        self.pe_last.then_inc(self.sem["pe"], 1)
        tok = (self.sem["pe"], self.cnt["pe"], "pe")
        for reads, writes in self.pe_pending:
            self._record(tok, reads, writes)
        self.pe_pending = []
        self.pe_last = None

    def op(self, eng, fn, reads=(), writes=()):
        if eng != "pe":
            self._flush_pe()
        need = self._collect(eng, reads, writes)
        self._emit_waits(eng, need)
        inst = fn()
        if eng == "pe":
            self.pe_pending.append((list(reads), list(writes)))
            self.pe_last = inst
            return inst
        self.cnt[eng] += 1
        inst.then_inc(self.sem[eng], 1)
        tok = (self.sem[eng], self.cnt[eng], eng)
        self._record(tok, reads, writes)
        return inst

    def dma(self, q, out, in_, reads=(), writes=(), chan="d", accumulate=False):
        self._flush_pe()
        need = self._collect(q, reads, writes)
        self._emit_waits(q, need)
        ch = self._chan(chan + "_" + q)
        inst = self.eng[q].dma_start(out=out, in_=in_)
        ch[1] += 16
        inst.then_inc(ch[0], 16)
        tok = (ch[0], ch[1], "dma")
        self._record(tok, reads, writes, accumulate=accumulate)
        return inst

    def wait_all(self, eng, deps):
        self._flush_pe()
        need = self._collect(eng, deps, ())
        self._emit_waits(eng, need)

    def barrier(self):
        self._flush_pe()
        for e in ("pe", "act", "dve", "pool", "sp"):
            need = {}
            for f in ("pe", "act", "dve", "pool"):
                if f != e and self.cnt[f] > 0:
                    need[id(self.sem[f])] = (self.sem[f], self.cnt[f])
            for name, (s, v) in self.chan.items():
                if v > 0:
                    need[id(s)] = (s, v)
            self._emit_waits(e, need)


def t5_bucket(dist):
    max_exact = N_BUCKETS // 2
    d = np.maximum(dist, 1).astype(np.float64)
    large = max_exact + (np.log(d / max_exact) / math.log(MAX_DISTANCE / max_exact)
                         * (N_BUCKETS - max_exact)).astype(np.int32)
    large = np.minimum(large, N_BUCKETS - 1)
    return np.where(dist < max_exact, dist, large).astype(np.int32)


def host_constants():
    c = {}
    eye = np.eye(128, dtype=np.float32)
    c["c_ident_f"] = eye
    c["c_ident_b"] = eye.astype(ml_dtypes.bfloat16)
    k = np.arange(128)
    c["c_tri"] = (k[:, None] <= k[None, :]).astype(np.float32)
    c["c_ones_f"] = np.ones((128, 128), np.float32)
    c["c_nones_f"] = -np.ones((128, 128), np.float32)
    c["c_ones_b"] = np.ones((128, 128), ml_dtypes.bfloat16)
    blk = np.zeros((128, 128), np.float32)
    blk[:64, :64] = 1.0 / 64
    blk[64:, 64:] = 1.0 / 64
    c["c_blk64"] = blk.astype(ml_dtypes.bfloat16)
    c["c_mu"] = np.where(k[None, :] >= k[:, None], 0.0, NEG).astype(ml_dtypes.bfloat16)
    c["c_ml"] = np.where(k[:, None] > k[None, :], 0.0, NEG).astype(ml_dtypes.bfloat16)
    oh = np.zeros((3, 32, 384), np.float32)
    for pi, (win, r) in enumerate(PATTERNS):
        s = np.arange(0, 129)
        b = t5_bucket(s * r)
        for si, bi in zip(s, b):
            oh[pi, bi, si + 127] = 1.0
    c["c_onehot"] = oh
    selA = np.zeros((128, 128), np.float32)
    selA[64, 0:64] = 1.0
    selB = np.zeros((128, 128), np.float32)
    selB[0, 64:128] = 1.0
    c["c_selA"] = selA.astype(ml_dtypes.bfloat16)
    c["c_selB"] = selB.astype(ml_dtypes.bfloat16)
    return c


CONST_SHAPES = {
    "c_ident_f": ([128, 128], F32), "c_ident_b": ([128, 128], BF16), "c_tri": ([128, 128], F32),
    "c_ones_f": ([128, 128], F32), "c_nones_f": ([128, 128], F32), "c_ones_b": ([128, 128], BF16),
    "c_blk64": ([128, 128], BF16), "c_mu": ([128, 128], BF16), "c_ml": ([128, 128], BF16),
    "c_onehot": ([3, 32, 384], F32), "c_selA": ([128, 128], BF16), "c_selB": ([128, 128], BF16),
}


def build(debug=False, do_dn=True, do_att=True):
    nc = bass.Bass("TRN2", target_bir_lowering=False)
    es = ExitStack()
    T = Tracker(nc, es)
    es.enter_context(nc.allow_non_contiguous_dma(reason="small params / strided layouts"))
    dt = nc.dram_tensor
    x_d = dt("x", [S, D], F32, kind="ExternalInput").ap()
    normw_d = dt("norm_w", [D], F32, kind="ExternalInput").ap()
    win_d = dt("w_in", [D, D_IN], F32, kind="ExternalInput").ap()
    convw_d = dt("conv_w", [4, 1536], F32, kind="ExternalInput").ap()
    alog_d = dt("a_log", [4], F32, kind="ExternalInput").ap()
    dtb_d = dt("dt_bias", [4], F32, kind="ExternalInput").ap()
    dnw_d = dt("dn_norm_w", [128], F32, kind="ExternalInput").ap()
    qnw_d = dt("q_norm_w", [64], F32, kind="ExternalInput").ap()
    knw_d = dt("k_norm_w", [64], F32, kind="ExternalInput").ap()
    relb_d = dt("rel_bias", [8, 32], F32, kind="ExternalInput").ap()
    wout_d = dt("w_out", [D, D], F32, kind="ExternalInput").ap()
    cst_d = {n: dt(n, sh, ty, kind="ExternalInput").ap() for n, (sh, ty) in CONST_SHAPES.items()}
    out_d = dt("out", [S, D], F32, kind="ExternalOutput").ap()
    vscr_d = dt("vscr", [S, 512], BF16, kind="Internal").ap()
    mix_d = dt("mixscr", [D, S], BF16, kind="Internal").ap()
    gscr_d = dt("gscr", [24, 128 * 384], BF16, kind="Internal").ap()
    dbg_d = {}
    if debug:
        dbg_d["mix"] = dt("dbg_mix", [D, S], BF16, kind="ExternalOutput").ap()

    uniq = [0]

    def sb(name, shape, ty, stack=None):
        uniq[0] += 1
        return (stack or es).enter_context(nc.sbuf_tensor("%s_%d" % (name, uniq[0]), shape, ty))

    A = T.op
    ACT = lambda **kw: nc.scalar.activation(**kw)

    psbig = [es.enter_context(nc.psum_tensor("psbig%d" % i, [128, 1024], F32)) for i in range(2)]
    psf = [psbig[0][:, 0:512], psbig[0][:, 512:1024], psbig[1][:, 0:512], psbig[1][:, 512:1024]]
    psf += [es.enter_context(nc.psum_tensor("psf%d" % i, [128, 512], F32))[:, :] for i in range(2)]
    psb = [es.enter_context(nc.psum_tensor("psb%d" % i, [128, 1024], BF16)) for i in range(2)]
    psf_dep = [Dep("psf%d" % i, excl=True) for i in range(6)]
    psb_dep = [Dep("psb%d" % i, excl=True) for i in range(2)]
    rot = {"f": 0, "b": 0}

    def PSF():
        i = rot["f"]
        rot["f"] = (i + 1) % 6
        return psf[i], psf_dep[i]

    def PSB():
        i = rot["b"]
        rot["b"] = (i + 1) % 2
        return psb[i], psb_dep[i]

    fill_ap = psb[1][:, :].bitcast(F32)

    def filler(n):
        for _ in range(n):
            nc.tensor.matmul(fill_ap[:, 0:128], cst["c_ident_b"][:], cst["c_ident_b"][:], start=True, stop=True)

    cst = {}
    cdep = Dep("consts")
    for n, (sh, ty) in CONST_SHAPES.items():
        if n == "c_onehot":
            continue
        cst[n] = sb("s_" + n, sh, ty)
        T.dma("sp", cst[n][:], cst_d[n][:, :], writes=[cdep], chan="const", accumulate=True)
    ident_f, ident_b = cst["c_ident_f"], cst["c_ident_b"]
    qnw = sb("qnw", [128, 1], F32)
    knw = sb("knw", [128, 1], F32)
    for hh in range(2):
        T.dma("sp", qnw[hh * 64:(hh + 1) * 64, :], qnw_d.rearrange("(p o) -> p o", o=1), writes=[cdep],
              chan="const", accumulate=True)
        T.dma("sp", knw[hh * 64:(hh + 1) * 64, :], knw_d.rearrange("(p o) -> p o", o=1), writes=[cdep],
              chan="const", accumulate=True)
    dnw = sb("dnw", [128, 1], F32)
    T.dma("sp", dnw[:], dnw_d.rearrange("(p o) -> p o", o=1), writes=[cdep], chan="const", accumulate=True)
    relbT = sb("relbT", [32, 8], F32)
    T.dma("sp", relbT[:], relb_d.rearrange("h b -> b h"), writes=[cdep], chan="const", accumulate=True)
    convw = sb("convw", [128, 12, 4], F32)
    for i in range(4):
        T.dma("sp", convw[:, :, i], convw_d[i, :].rearrange("(t p) -> p t", p=128), writes=[cdep], chan="const",
              accumulate=True)
    alog_bc = sb("alog_bc", [128, 4], F32)
    dtb_bc = sb("dtb_bc", [128, 4], F32)
    T.dma("sp", alog_bc[:], alog_d.partition_broadcast(128), writes=[cdep], chan="const", accumulate=True)
    T.dma("sp", dtb_bc[:], dtb_d.partition_broadcast(128), writes=[cdep], chan="const", accumulate=True)
    qnw_s = sb("qnw_s", [128, 1], F32)
    qnw_dep = Dep()
    A("dve", lambda: nc.vector.tensor_scalar(out=qnw_s[:], in0=qnw[:], scalar1=0.125, scalar2=None, op0=ALU.mult),
      reads=[cdep], writes=[qnw_dep])

    qkg_d = dt("qkgscr", [3, 512, S], BF16, kind="Internal").ap()
    mix_dep = Dep("mix")
    vscr_dep = Dep("vscr")
    qkg_dep = Dep("qkg")
    win_v = win_d.rearrange("(c p) n -> p c n", p=128)

    wl_i = [0]

    def make_loader(stack):
        def load_w(dst, dst_dep, c0, n, src_v=None, scale=True, q="pool"):
            src_ = (src_v if src_v is not None else win_v)
            wl_i[0] += 1
            for c in range(8):
                T.dma("pool", dst[:, c, 0:n], src_[:, c, c0:c0 + n], writes=[dst_dep], chan="wl%d" % wl_i[0],
                      accumulate=True)
        return load_w

    def zero_mix(r0, r1):
        with ExitStack() as z:
            zt = sb("zt", [128, S], BF16, z)
            zd = Dep()
            A("pool", lambda: nc.gpsimd.memset(zt[:], 0.0), writes=[zd])
            for r in range(r0, r1, 128):
                T.dma("sp", mix_d[r:r + 128, :], zt[:], reads=[zd], writes=[mix_dep], chan="zmix", accumulate=True)
            T.barrier()

    with ExitStack() as pH:
        hT = sb("hT", [128, 8, S], BF16, pH)
        hT_dep = [Dep("hT%d" % i) for i in range(NT)]
        load_w0 = make_loader(pH)
        wdn = sb("wdn", [128, 8, 2056], BF16, pH)
        wdn_dep = Dep("wdn")
        pW = ExitStack()
        wb = [sb("wb%d" % i, [128, 8, 512], BF16, pW) for i in range(2)]
        wb_dep = [Dep() for _ in range(2)]
        if do_att:
            load_w0(wb[0], wb_dep[0], 2056, 512)
            load_w0(wb[1], wb_dep[1], 2056 + 512, 512)
        if do_dn:
            for g4 in range(4):
                load_w0(wdn[:, :, g4 * 512:(g4 + 1) * 512], wdn_dep, g4 * 512, 512)
            load_w0(wdn[:, :, 2048:2056], wdn_dep, 2048, 8)
        g_dep = Dep("gscr")
        if do_att:
            onehot = sb("onehot", [32, 3, 384], F32, pW)
            T.dma("sp", onehot[:], cst_d["c_onehot"].rearrange("p b i -> b p i"), writes=[cdep], chan="const_oh",
                  accumulate=True)
            expb = sb("expb", [32, 8], F32, pW)
            expb_dep = Dep()
            A("act", lambda: ACT(out=expb[:], in_=relbT[:], func=AF.Exp), reads=[cdep], writes=[expb_dep])
            ebc = sb("ebc", [32, 8, 128], F32, pW)
            ebc_dep = Dep()
            for h in range(8):
                A("dve", lambda h=h: nc.vector.tensor_scalar(out=ebc[:, h, :], in0=cst["c_ones_f"][0:32, :],
                                                             scalar1=expb[:, h:h + 1], scalar2=None, op0=ALU.mult),
                  reads=[expb_dep, cdep], writes=[ebc_dep])
            rowrep = [sb("rowrep%d" % i, [128, 384], BF16, pW) for i in range(2)]
            rr_dep = [Dep() for _ in range(2)]
            for pi in range(3):
                for h in range(8):
                    idx = pi * 8 + h
                    ps, psd = PSF()
                    A("pe", lambda: nc.tensor.matmul(ps[:, 0:384], ebc[:, h, :], onehot[:, pi, :], start=True, stop=True),
                      reads=[ebc_dep, cdep], writes=[psd])
                    rb = idx % 2
                    A("dve", lambda: nc.vector.tensor_copy(out=rowrep[rb][:], in_=ps[:, 0:384]), reads=[psd],
                      writes=[rr_dep[rb]])
                    T.dma("sp", gscr_d[idx, :].rearrange("(p i) -> p i", i=384), rowrep[rb][:], reads=[rr_dep[rb]],
                          writes=[g_dep], chan="rr%d" % rb, accumulate=True)
        if True:
            p1 = pW
            normw_bc = sb("normw_bc", [128, D], F32, p1)
            T.dma("sp", normw_bc[:], normw_d.partition_broadcast(128), writes=[cdep], chan="const_nw", accumulate=True)
            xt = [sb("xt%d" % i, [128, D], F32, p1) for i in range(3)]
            xt_dep = [Dep() for _ in range(3)]
            junk = sb("junk", [128, D], BF16, p1)
            junk_dep = Dep()
            xs = [sb("xs%d" % i, [128, D], BF16, p1) for i in range(2)]
            xs_dep = [Dep() for _ in range(2)]
            junk2 = sb("junk2", [128, D], BF16, p1)
            ss = sb("ss", [128, NT], F32, p1)
            rs = sb("rs", [128, NT], F32, p1)
            ss_dep = [Dep() for _ in range(NT)]
            def p1_tile(tt):
                b3 = tt % 3
                b2 = tt % 2
                T.dma("sp", xt[b3][:], x_d[tt * 128:(tt + 1) * 128, :], writes=[xt_dep[b3]], chan="xt%d" % b3)
                A("act", lambda: ACT(out=junk[:], in_=xt[b3][:], func=AF.Square, accum_out=ss[:, tt:tt + 1]),
                  reads=[xt_dep[b3]], writes=[junk_dep, ss_dep[tt]])
                A("dve", lambda: nc.vector.tensor_scalar(out=rs[:, tt:tt + 1], in0=ss[:, tt:tt + 1], scalar1=1.0 / D,
                                                         scalar2=EPS, op0=ALU.mult, op1=ALU.add),
                  reads=[ss_dep[tt]], writes=[ss_dep[tt]])
                A("act", lambda: ACT(out=rs[:, tt:tt + 1], in_=rs[:, tt:tt + 1], func=AF.Sqrt),
                  reads=[ss_dep[tt]], writes=[ss_dep[tt]])
                A("dve", lambda: nc.vector.reciprocal(out=rs[:, tt:tt + 1], in_=rs[:, tt:tt + 1]),
                  reads=[ss_dep[tt]], writes=[ss_dep[tt]])
                A("dve", lambda: nc.vector.scalar_tensor_tensor(out=xs[b2][:], in0=xt[b3][:], scalar=rs[:, tt:tt + 1],
                                                                in1=normw_bc[:], op0=ALU.mult, op1=ALU.mult),
                  reads=[xt_dep[b3], ss_dep[tt], cdep], writes=[xs_dep[b2]])
            def p1_tileB(tt):
                b2 = tt % 2
                pt, ptd = PSB()
                for c in range(8):
                    A("pe", lambda c=c: nc.tensor.transpose(pt[:, c * 128:(c + 1) * 128],
                                                            xs[b2][:, c * 128:(c + 1) * 128], ident_b[:]),
                      reads=[xs_dep[b2], cdep], writes=[ptd])
                A("act", lambda: nc.scalar.copy(out=hT[:, :, tt * 128:(tt + 1) * 128],
                                                in_=pt[:, :].rearrange("p (c t) -> p c t", c=8)),
                  reads=[ptd], writes=[hT_dep[tt]])
        if do_att:
            start_group, proj_step, v_step = att_proj(nc, T, locals(), pW)
        p1_tile(0)
        for tg in range(8):
            for tt in range(tg * 4, tg * 4 + 4):
                if tt + 1 < NT:
                    p1_tile(tt + 1)
                p1_tileB(tt)
            if do_att and tg >= 1:
                proj_step(0, tg - 1)
        if do_att:
            proj_step(0, 7)
            for gi in (1, 2):
                start_group(gi)
                for tg in range(8):
                    proj_step(gi, tg)
            start_group(3)
            for tt in range(NT):
                v_step(tt)
        T.barrier()
        pW.close()
        if do_dn:
            phase_dn(nc, T, locals())
        T.barrier()
    if not do_dn:
        zero_mix(0, 512)
    wo = sb("wo", [128, 8, D], BF16)
    wo_dep = Dep()
    wout_v = wout_d.rearrange("(c p) n -> p c n", p=128)
    load_wo = make_loader(es)

    def issue_wo_loads():
        for g in range(2):
            load_wo(wo[:, :, g * 512:(g + 1) * 512], wo_dep, g * 512, 512, src_v=wout_v)

    if do_att:
        att_compute(nc, T, locals())
    else:
        issue_wo_loads()
        zero_mix(512, 1024)

    with ExitStack() as p5:
        mixT = sb("mixT", [128, 8, S], BF16, p5)
        mixT_dep = Dep()
        for c in range(8):
            T.dma("sp" if c % 2 == 0 else "pool", mixT[:, c, :], mix_d[c * 128:(c + 1) * 128, :], reads=[mix_dep],
                  writes=[mixT_dep], chan="mixT", accumulate=True)
            if debug:
                pass
        xr = [sb("xr%d" % i, [128, D], F32, p5) for i in range(3)]
        xr_dep = [Dep() for _ in range(3)]
        ot = [sb("ot%d" % i, [128, D], F32, p5) for i in range(2)]
        ot_dep = [Dep() for _ in range(2)]
        out_dep = Dep("out")
        for tt in range(NT):
            b3 = tt % 3
            b2 = tt % 2
            T.dma("pool", xr[b3][:], x_d[tt * 128:(tt + 1) * 128, :], writes=[xr_dep[b3]], chan="xr%d" % b3)
            for half in range(2):
                ps, psd = PSF()
                for c in range(8):
                    A("pe", lambda c=c: nc.tensor.matmul(ps[:, :], mixT[:, c, tt * 128:(tt + 1) * 128],
                                                         wo[:, c, half * 512:(half + 1) * 512], start=(c == 0),
                                                         stop=(c == 7)),
                      reads=[mixT_dep, wo_dep], writes=[psd])
                A("dve", lambda: nc.vector.tensor_tensor(out=ot[b2][:, half * 512:(half + 1) * 512], in0=ps[:, :],
                                                         in1=xr[b3][:, half * 512:(half + 1) * 512], op=ALU.add),
                  reads=[psd, xr_dep[b3]], writes=[ot_dep[b2]])
            T.dma("sp", out_d[tt * 128:(tt + 1) * 128, :], ot[b2][:], reads=[ot_dep[b2]], writes=[out_dep],
                  chan="ot%d" % b2, accumulate=True)
        if debug:
            for c in range(8):
                T.dma("sp", dbg_d["mix"][c * 128:(c + 1) * 128, :], mixT[:, c, :], reads=[mixT_dep], writes=[out_dep],
                      chan="dbgmix", accumulate=True)
        T.wait_all("sp", [out_dep])
    T.barrier()
    es.close()
    return nc


def att_proj(nc, T, E, pa):
    sb, A, ACT, PSF = E["sb"], E["A"], E["ACT"], E["PSF"]
    hT, hT_all, cst, cdep = E["hT"], E["hT_dep"], E["cst"], E["cdep"]
    vscr_d, qkg_d, vscr_dep, qkg_dep = E["vscr_d"], E["qkg_d"], E["vscr_dep"], E["qkg_dep"]
    qnw_s, qnw_dep, knw = E["qnw_s"], E["qnw_dep"], E["knw"]
    C0 = 2056
    load_w = E["make_loader"](pa)
    wb, wb_dep = E["wb"], E["wb_dep"]
    sq = [sb("sq%d" % i, [128, 512], BF16, pa) for i in range(2)]
    sq_dep = [Dep() for _ in range(2)]
    ln = [sb("ln%d" % i, [128, 512], F32, pa) for i in range(2)]
    ln_dep = [Dep() for _ in range(2)]
    vst = [sb("vst%d" % i, [128, 512], BF16, pa) for i in range(3)]
    vst_dep = [Dep() for _ in range(3)]
    k2 = [0]
    k3 = [0]
    groups = ((C0, "q"), (C0 + 512, "k"), (C0 + 1536, "g"), (C0 + 1024, "v"))

    def start_group(gi):
        if gi >= 2:
            load_w(wb[gi % 2], wb_dep[gi % 2], groups[gi][0], 512)

    def v_step(tt):
        wbi = 1
        ps, psd = PSF()
        for c in range(8):
            A("pe", lambda c=c: nc.tensor.matmul(ps[:, :], hT[:, c, tt * 128:(tt + 1) * 128], wb[wbi][:, c, :],
                                                 start=(c == 0), stop=(c == 7)),
              reads=[hT_all[tt], wb_dep[wbi]], writes=[psd])
        b = k3[0] % 3
        k3[0] += 1
        A("act", lambda: nc.scalar.copy(out=vst[b][:], in_=ps[:, :]), reads=[psd], writes=[vst_dep[b]])
        T.dma("sp", vscr_d[tt * 128:(tt + 1) * 128, :], vst[b][:], reads=[vst_dep[b]], writes=[vscr_dep],
              chan="vst%d" % b, accumulate=True)

    def proj_step(gi, tg):
        c0, kind = groups[gi]
        wbi = gi % 2
        ki = {"q": 0, "k": 1, "g": 2}[kind]
        for ct in range(4):
            ps, psd = PSF()
            for c in range(8):
                A("pe", lambda c=c: nc.tensor.matmul(ps[:, :], wb[wbi][:, c, ct * 128:(ct + 1) * 128],
                                                     hT[:, c, tg * 512:(tg + 1) * 512], start=(c == 0), stop=(c == 7)),
                  reads=hT_all[tg * 4:tg * 4 + 4] + [wb_dep[wbi]], writes=[psd])
            ob = k3[0] % 3
            k3[0] += 1
            if kind == "g":
                A("act", lambda: ACT(out=vst[ob][:], in_=ps[:, :], func=AF.Silu), reads=[psd], writes=[vst_dep[ob]])
            else:
                b = k2[0] % 2
                k2[0] += 1
                A("act", lambda: ACT(out=sq[b][:], in_=ps[:, :], func=AF.Square), reads=[psd], writes=[sq_dep[b]])
                ps2, ps2d = PSF()
                A("pe", lambda: nc.tensor.matmul(ps2[:, :], cst["c_blk64"][:], sq[b][:], start=True, stop=True),
                  reads=[sq_dep[b], cdep], writes=[ps2d])
                A("act", lambda: ACT(out=ln[b][:], in_=ps2[:, :], func=AF.Ln, bias=EPS), reads=[ps2d],
                  writes=[ln_dep[b]])
                A("act", lambda: ACT(out=ln[b][:], in_=ln[b][:], func=AF.Exp, scale=-0.5), reads=[ln_dep[b]],
                  writes=[ln_dep[b]])
                nw = qnw_s if kind == "q" else knw
                A("dve", lambda: nc.vector.scalar_tensor_tensor(out=vst[ob][:], in0=ps[:, :], scalar=nw[:, 0:1],
                                                                in1=ln[b][:], op0=ALU.mult, op1=ALU.mult),
                  reads=[psd, ln_dep[b], qnw_dep, cdep], writes=[vst_dep[ob]])
            T.dma("sp", qkg_d[ki, ct * 128:(ct + 1) * 128, tg * 512:(tg + 1) * 512], vst[ob][:],
                  reads=[vst_dep[ob]], writes=[qkg_dep], chan="vst%d" % ob, accumulate=True)

    return start_group, proj_step, v_step


def att_compute(nc, T, E):
    sb, A, ACT = E["sb"], E["A"], E["ACT"]
    psf, psf_dep = E["psf"], E["psf_dep"]
    cst, cdep = E["cst"], E["cdep"]
    vscr_d, qkg_d, vscr_dep, qkg_dep = E["vscr_d"], E["qkg_d"], E["vscr_dep"], E["qkg_dep"]
    mix_d, mix_dep, gscr_d = E["mix_d"], E["mix_dep"], E["gscr_d"]
    relbT = E["relbT"]
    cst_d = E["cst_d"]
    PSF = E["PSF"]
    with ExitStack() as pa:
        Et = sb("Et", [128, 24, 256], BF16, pa)
        Et_dep = Dep("Et")
        Et_deps = [Dep("Et%d" % i) for i in range(24)]
        g_dep = E["g_dep"]

        qkg = [sb("qkg%d" % i, [128, 3, S], BF16, pa) for i in range(2)]
        qkgs_dep = [Dep() for _ in range(2)]
        accs = [[sb("acc%d_%d" % (p_, i), [128, S], F32, pa) for i in range(2)] for p_ in range(2)]
        accs_dep = [[Dep() for _ in range(2)] for _ in range(2)]
        dhl = sb("dhl", [128, 2, S], BF16, pa)
        dhl_dep = Dep()
        slab = [sb("slab%d" % i, [128, 32, 192], BF16, pa) for i in range(2)]
        slab_dep = [Dep() for _ in range(2)]
        for i in range(2):
            A("pool", lambda i=i: nc.gpsimd.memset(slab[i][:], 0.0), writes=[slab_dep[i]])
            A("pool", lambda i=i: nc.gpsimd.memset(slab[i][:, :, 64:65], 1.0), writes=[slab_dep[i]])
        NB = 6
        import os
        DEPTH = int(os.environ.get('K_DEPTH', '4'))
        for p_ in range(2):
            for i in range(2):
                A("dve" if p_ == 0 else "pool",
                  (lambda: nc.vector.memset(accs[p_][i][:], 0.0)) if p_ == 0 else
                  (lambda: nc.gpsimd.memset(accs[p_][i][:], 0.0)), writes=[accs_dep[p_][i]])
        PREF = int(os.environ.get('K_PREF', '1'))
        pT = [sb("pT%d" % i, [128, 2, 256], BF16, pa) for i in range(NB)]
        pT_dep = [Dep() for _ in range(NB)]
        eT = [sb("eT%d" % i, [128, 2, 256], BF16, pa) for i in range(NB)]
        eT_dep = [Dep() for _ in range(NB)]
        rden = [sb("rden%d" % i, [128, 512], F32, pa) for i in range(2)]
        rden_dep = [Dep() for _ in range(2)]
        yst = [sb("yst%d" % i, [128, 512], BF16, pa) for i in range(2)]
        yst_dep = [Dep() for _ in range(2)]
        y_i = [0]
        rot_s = [0]
        rot_o = [0]

        psbig = E["psbig"]
        big_dep = [Dep(excl=True) for _ in range(2)]

        def PS_S():
            i = rot_s[0]
            rot_s[0] = (i + 1) % 2
            return psbig[i], big_dep[i]

        po_banks = [(psf[4], psf_dep[4]), (psf[5], psf_dep[5]),
                    (E["psb"][0][:, :].bitcast(F32), E["psb_dep"][0]), (E["psb"][1][:, :].bitcast(F32), E["psb_dep"][1])]

        def PS_O():
            i = rot_o[0]
            rot_o[0] = (rot_o[0] + 1) % 4
            return po_banks[i]

        def load_pair(hp):
            b = hp % 2
            for ki in range(3):
                T.dma("sp", qkg[b][:, ki, :], qkg_d[ki, hp * 128:(hp + 1) * 128, :], reads=[qkg_dep],
                      writes=[qkgs_dep[b]], chan="qkg%d" % b, accumulate=True)

        def load_slab(hp, pi, si):
            r = PATTERNS[pi][1]
            nblk = 32 // r
            sl = slab[si]
            vv = vscr_d.rearrange("(n p r) c -> r p n c", p=128, r=r)
            for res in range(r):
                for hh in range(2):
                    col = (2 * hp + hh) * 64
                    dcol = 0 if hh == 0 else 128
                    T.dma("sp", sl[:, res * nblk:(res + 1) * nblk, dcol:dcol + 64], vv[res, :, :, col:col + 64],
                          reads=[vscr_dep], writes=[slab_dep[si]], chan="slab%d" % si, accumulate=True)

        units = []
        for hp in range(4):
            for pi, (win, r) in enumerate(PATTERNS):
                nblk = 32 // r
                for res in range(r):
                    for n in range(nblk):
                        units.append((hp, pi, r, nblk, res, n))
        state = {}

        def stage12(u, k):
            hp, pi, r, nblk, res, n = u
            gi = hp * 3 + pi
            si = gi % 2
            li = res * nblk + n
            if li == DEPTH + 1:
                if gi + 1 < 12:
                    load_slab((gi + 1) // 3, (gi + 1) % 3, (gi + 1) % 2)
                if pi == 1 and hp + 1 < 4:
                    load_pair(hp + 1)
            qb = hp % 2
            qT = qkg[qb][:, 0, :]
            kT = qkg[qb][:, 1, :]
            qd = qkgs_dep[qb]
            t0 = res + r * 128 * n
            ncol = 256 if n + 1 < nblk else 128
            ksl = slice(t0, t0 + 127 * r + 1, r)
            qsl = slice(t0, t0 + (ncol - 1) * r + 1, r)
            ps, psd = PS_S()
            for hh in range(2):
                pb = hh * 64
                A("pe", lambda: nc.tensor.matmul(ps[:, hh * 512:hh * 512 + ncol], kT[pb:pb + 64, ksl], qT[pb:pb + 64, qsl],
                                                 start=True, stop=True), reads=[qd], writes=[psd])
            b = k % NB
            psv = ps[:, :].rearrange("p (h c) -> p h c", h=2)
            A("act", lambda: ACT(out=eT[b][:, :, 0:ncol], in_=psv[:, :, 0:ncol], func=AF.Exp), reads=[psd],
              writes=[eT_dep[b]])
            MP = int(os.environ.get("K_MULTPOOL", "2"))
            usep = True if MP == 0 else ((k % 3 != 0) if MP == 3 else (k % 2 == 0))
            eng_ = "pool" if usep else "dve"
            E_ = nc.gpsimd if usep else nc.vector
            if ncol == 256:
                fl = lambda t: t.rearrange("p h c -> p (h c)")
                A(eng_, lambda: E_.tensor_tensor(out=fl(pT[b][:, :, :]), in0=fl(eT[b][:, :, :]),
                                                 in1=fl(Et[:, pi * 8 + 2 * hp:pi * 8 + 2 * hp + 2, :]), op=ALU.mult),
                  reads=[eT_dep[b], Et_deps[pi * 8 + 2 * hp], Et_deps[pi * 8 + 2 * hp + 1]], writes=[pT_dep[b]])
            else:
                A(eng_, lambda: E_.tensor_tensor(out=pT[b][:, :, 0:ncol], in0=eT[b][:, :, 0:ncol],
                                                 in1=Et[:, pi * 8 + 2 * hp:pi * 8 + 2 * hp + 2, 0:ncol], op=ALU.mult),
                  reads=[eT_dep[b], Et_deps[pi * 8 + 2 * hp], Et_deps[pi * 8 + 2 * hp + 1]], writes=[pT_dep[b]])

        def stage3(u, k):
            hp, pi, r, nblk, res, n = u
            gi = hp * 3 + pi
            si = gi % 2
            sl = slab[si]
            ti = res * nblk + n
            t0 = res + r * 128 * n
            ncol = 256 if n + 1 < nblk else 128
            qsl = slice(t0, t0 + (ncol - 1) * r + 1, r)
            b = k % NB
            acc, acc_dep = accs[hp % 2], accs_dep[hp % 2]
            po, pod = PS_O()
            for hh in range(2):
                lc = 0 if hh == 0 else 64
                A("pe", lambda: nc.tensor.matmul(po[:, hh * 256:hh * 256 + ncol], sl[:, ti, lc:lc + 128],
                                                 pT[b][:, hh, 0:ncol], start=True, stop=True),
                  reads=[slab_dep[si], pT_dep[b]], writes=[pod])
            for hh in range(2):
                M = 65 if hh == 0 else 128
                A("dve", lambda: nc.vector.tensor_tensor(out=acc[hh][0:M, qsl], in0=po[0:M, hh * 256:hh * 256 + ncol],
                                                         in1=acc[hh][0:M, qsl], op=ALU.add),
                  reads=[pod, acc_dep[hh]], writes=[acc_dep[hh]])
            if pi == 2 and res == r - 1 and n == nblk - 1:
                fin_bg.append(finish_pair(hp))

        fin_bg = []

        def finish_pair(hp):
            qb = hp % 2
            gT = qkg[qb][:, 2, :]
            qd = qkgs_dep[qb]
            acc, acc_dep = accs[hp % 2], accs_dep[hp % 2]
            for hh, row in ((0, 64), (1, 0)):
                A("dve", lambda: nc.vector.tensor_copy(out=dhl[row:row + 1, 0, :], in_=acc[hh][row:row + 1, :]),
                  reads=[acc_dep[hh]], writes=[dhl_dep])
                yield
                A("dve", lambda: nc.vector.tensor_tensor(out=dhl[row:row + 1, 1, :], in0=acc[hh][row:row + 1, :],
                                                         in1=dhl[row:row + 1, 0, :], op=ALU.subtract),
                  reads=[acc_dep[hh], dhl_dep], writes=[dhl_dep])
                yield
            for tg in range(8):
                cs = slice(tg * 512, (tg + 1) * 512)
                yb = y_i[0] % 2
                y_i[0] += 1
                for hh in range(2):
                    pr = slice(0, 64) if hh == 0 else slice(64, 128)
                    ps, psd = PS_O()
                    for hl in range(2):
                        if hh == 0:
                            A("pe", lambda: nc.tensor.matmul(ps[0:64, :], cst["c_selA"][64:65, 0:64], dhl[64:65, hl, cs],
                                                             start=(hl == 0), stop=(hl == 1)),
                              reads=[dhl_dep, cdep], writes=[psd])
                        else:
                            A("pe", lambda: nc.tensor.matmul(ps[:, :], cst["c_selB"][0:1, :], dhl[0:1, hl, cs],
                                                             start=(hl == 0), stop=(hl == 1)),
                              reads=[dhl_dep, cdep], writes=[psd])
                    A("act", lambda: ACT(out=rden[hh][pr, :], in_=ps[pr, :], func=AF.Ln), reads=[psd],
                      writes=[rden_dep[hh]])
                    A("act", lambda: ACT(out=rden[hh][pr, :], in_=rden[hh][pr, :], func=AF.Exp, scale=-1.0),
                      reads=[rden_dep[hh]], writes=[rden_dep[hh]])
                    A("pool", lambda: nc.gpsimd.tensor_tensor(out=rden[hh][pr, :], in0=rden[hh][pr, :], in1=gT[pr, cs],
                                                              op=ALU.mult), reads=[rden_dep[hh], qd],
                      writes=[rden_dep[hh]])
                    A("dve", lambda: nc.vector.tensor_tensor(out=yst[yb][pr, :], in0=acc[hh][pr, cs], in1=rden[hh][pr, :],
                                                             op=ALU.mult), reads=[rden_dep[hh], acc_dep[hh]],
                      writes=[yst_dep[yb]])
                    yield
                T.dma("sp", mix_d[512 + hp * 128:512 + (hp + 1) * 128, cs], yst[yb][:], reads=[yst_dep[yb]],
                      writes=[mix_dep], chan="yst%d" % yb, accumulate=True)
            if hp + 2 < 4:
                for i in range(2):
                    A("pool", lambda: nc.gpsimd.memset(acc[i][:], 0.0), writes=[acc_dep[i]])
            yield

        load_pair(0)
        load_slab(0, 0, 0)
        n_et = 0
        for hp_ in range(4):
            for pi_ in range(3):
                for hh_ in range(2):
                    idx = pi_ * 8 + 2 * hp_ + hh_
                    src_ap = bass.AP(tensor=gscr_d.tensor, offset=idx * 128 * 384 + 127,
                                     ap=[[383, 128], [128, 2], [1, 128]])
                    T.dma("sp" if n_et % 2 == 0 else "act", Et[:, idx, :].rearrange("p (a b) -> p a b", a=2), src_ap,
                          reads=[g_dep], writes=[Et_deps[idx]], chan="et%d" % idx)
                    n_et += 1
        nU = min(len(units), int(os.environ.get('K_NU', '100000')))
        FILL_ATT = int(os.environ.get("K_FILL_ATT", "0"))
        for i in range(nU + DEPTH):
            if i >= DEPTH:
                stage3(units[i - DEPTH], i - DEPTH)
            E["filler"](FILL_ATT)
            if i < nU:
                stage12(units[i], i)
            if i == 40:
                E["issue_wo_loads"]()
            nxt = []
            for g_ in fin_bg:
                try:
                    next(g_)
                    nxt.append(g_)
                except StopIteration:
                    pass
            fin_bg[:] = nxt
        while fin_bg:
            nxt = []
            for g_ in fin_bg:
                try:
                    next(g_)
                    nxt.append(g_)
                except StopIteration:
                    pass
            fin_bg[:] = nxt
    T.barrier()


def phase_dn(nc, T, E):
    sb, A, ACT, PSF = E["sb"], E["A"], E["ACT"], E["PSF"]
    PSB = E["PSB"]
    filler = E["filler"]
    FILL_DN = int(os.environ.get("K_FILL_DN", "0"))
    hT, hT_all, cst, cdep = E["hT"], E["hT_dep"], E["cst"], E["cdep"]
    mix_d, mix_dep = E["mix_d"], E["mix_dep"]
    convw, alog_bc, dtb_bc, dnw = E["convw"], E["alog_bc"], E["dtb_bc"], E["dnw"]
    ident_f, ident_b = cst["c_ident_f"], cst["c_ident_b"]
    ones_f, nones_f, ones_b, tri = cst["c_ones_f"], cst["c_nones_f"], cst["c_ones_b"], cst["c_tri"]
    mu_b, ml_b = cst["c_mu"], cst["c_ml"]
    V = nc.vector
    G = nc.gpsimd

    def mm(out, lhsT, rhs, start, stop, reads, writes):
        A("pe", lambda: nc.tensor.matmul(out, lhsT, rhs, start=start, stop=stop), reads=reads, writes=writes)

    with ExitStack() as pd:
        wdn, wdn_dep = E["wdn"], E["wdn_dep"]
        diagc = sb("diagc", [128, 12, 4, 128], BF16, pd)
        diagc_dep = Dep()
        for t in range(12):
            for i in range(4):
                A("pool", lambda: G.tensor_scalar(out=diagc[:, t, i, :], in0=ident_f[:], scalar1=convw[:, t, i:i + 1],
                                                  scalar2=0.0, op0=ALU.mult, op1=ALU.add),
                  reads=[cdep], writes=[diagc_dep])
        raw = sb("raw", [128, 12, 515], BF16, pd)
        raw_dep = [Dep() for _ in range(12)]
        for t in range(12):
            A("pool", lambda: G.memset(raw[:, t, 0:3], 0.0), writes=[raw_dep[t]])
        Sb = sb("Sb", [128, 4, 128], BF16, pd)
        S_dep = Dep("S")
        A("pool", lambda: G.memset(Sb[:], 0.0), writes=[S_dep])
        dtb16 = sb("dtb16", [128, 4, 4], F32, pd)
        nA16 = sb("nA16", [128, 4, 4], F32, pd)
        c16_dep = Dep()
        nA = sb("nA", [128, 4], F32, pd)
        A("act", lambda: ACT(out=nA[:], in_=alog_bc[:], func=AF.Exp), reads=[cdep], writes=[c16_dep])
        for b in range(4):
            A("dve", lambda: V.tensor_scalar(out=nA16[:, b, :], in0=nA[:], scalar1=-1.0, scalar2=None, op0=ALU.mult),
              reads=[c16_dep], writes=[c16_dep])
            A("dve", lambda: V.tensor_copy(out=dtb16[:, b, :], in_=dtb_bc[:]), reads=[cdep], writes=[c16_dep])

        def t4(name, ty=BF16, w=128):
            return sb(name, [128, 4, w], ty, pd), Dep(name)

        GT = []
        for i in range(2):
            q_, qd_ = t4("qT4_%d" % i, BF16, 512)
            k_, kd_ = t4("kT4_%d" % i, BF16, 512)
            v_, vd_ = t4("vT4_%d" % i, BF16, 512)
            GT.append((q_, qd_, k_, kd_, v_, vd_))
        zsb = sb("zsb", [128, 512], BF16, pd)
        zsb_d = Dep()
        oT4, oT4_d = t4("oT4", F32, 512)
        sqb = sb("sqb", [128, 512], BF16, pd)
        sqb_d = Dep()
        lnb = sb("lnb", [128, 512], F32, pd)
        lnb_d = Dep()
        sqb2, sqb2_d, lnb2, lnb2_d = sqb, sqb_d, lnb, lnb_d
        rinv = sb("rinv", [128, 512], BF16, pd)
        rinv_d = Dep()
        rinvf = sb("rinvf", [128, 512], F32, pd)
        rinvf_d = Dep()
        yst = [sb("ydn%d" % i, [128, 512], BF16, pd) for i in range(2)]
        yst_d = [Dep() for _ in range(2)]
        ba = sb("ba", [128, 4, 8], F32, pd)
        ba_d = Dep()
        SM = []
        for i in range(2):
            sm_ = {n: sb("sm%d_" % i + n, [128, 4, 4], F32, pd) for n in
                   ("beta", "nbeta", "x1", "ax", "e", "l", "mx", "sp", "g", "gc", "ngc", "egc", "ebg", "dlt", "etail",
                    "gl")}
            SM.append((sm_, {n: Dep("sm_" + n) for n in sm_}))
        diag4, diag4_d = t4("diag4", F32)
        rgmu, rgmu_d = t4("rgmu", F32)
        rgml, rgml_d = t4("rgml", F32)
        mu4, mk4_d = t4("mu4", BF16)
        ml4, _ = t4("ml4", BF16)
        for h in range(4):
            A("pool", lambda: G.tensor_copy(out=mu4[:, h, :], in_=mu_b[:]), reads=[cdep], writes=[mk4_d])
            A("pool", lambda: G.tensor_copy(out=ml4[:, h, :], in_=ml_b[:]), reads=[cdep], writes=[mk4_d])
        DT4, DT4_d = t4("DT4")
        D4, D4_d = t4("D4")
        Eg4, Eg4_d = t4("Eg4")
        UW4, UW4_d = t4("UW4", BF16, 256)
        Mn4, Mn4_d = t4("Mn4")
        Qe4, Qe4_d = t4("Qe4")
        yi = [0]
        LNQ = -0.5 * math.log(128.0)
        f4 = lambda t: t[:].rearrange("p h w -> p (h w)")
        PRE = []
        for i in range(2):
            P = {}
            for n_, w_ in (("N4", 128), ("Y4", 128), ("ka4", 256), ("qd4", 128), ("vbk4", 256)):
                P[n_], P[n_ + "d"] = t4("%s_%d" % (n_, i), BF16, w_)
            PRE.append(P)
        XYR = [[(sb("xyr%d_%d" % (h, i), [128, 384], BF16, pd), Dep()) for i in range(2)] for h in range(4)]
        R6t = [(sb("r6_%d" % h, [128, 128], BF16, pd), Dep()) for h in range(4)]
        UWd = [Dep() for _ in range(4)]
        Mnd = [Dep() for _ in range(4)]
        Qed = [Dep() for _ in range(4)]
        psf, psf_dep = E["psf"], E["psf_dep"]
        rot2 = [0]

        pf2 = [(psf[4], psf_dep[4]), (psf[5], psf_dep[5]), (E["psb"][1][:, :].bitcast(F32), E["psb_dep"][1])]

        def PSF2():
            i = rot2[0]
            rot2[0] = (i + 1) % 3
            return pf2[i]

        PSB = lambda: (E["psb"][0], E["psb_dep"][0])
        if os.environ.get("K_FRONT_PSF2", "1") == "1":
            PSF = PSF2

        def group_fns(tg):
            ts_ = slice(tg * 512, (tg + 1) * 512)
            hdeps = hT_all[tg * 4:tg * 4 + 4]
            GTp = GT[tg % 2]
            qT4, qT4_d, kT4, kT4_d, vT4, vT4_d = GTp
            qs4, qs4_d, ks4, ks4_d = qT4, qT4_d, kT4, kT4_d
            sm, sm_d = SM[tg % 2]
            def gen_front():
                for ct in range(12):
                    ps, psd = PSF()
                    for c in range(8):
                        mm(ps[:, :], wdn[:, c, ct * 128:(ct + 1) * 128], hT[:, c, ts_], c == 0, c == 7,
                           hdeps + [wdn_dep], [psd])
                    if tg > 0:
                        A("pool", lambda: G.tensor_copy(out=raw[:, ct, 0:3], in_=raw[:, ct, 512:515]), reads=[raw_dep[ct]],
                          writes=[raw_dep[ct]])
                    A("dve", lambda: V.tensor_copy(out=raw[:, ct, 3:515], in_=ps[:, :]), reads=[psd], writes=[raw_dep[ct]])
                    ps2, ps2d = PSF()
                    for i in range(4):
                        mm(ps2[:, :], diagc[:, ct, i, :], raw[:, ct, i:i + 512], i == 0, i == 3, [diagc_dep, raw_dep[ct]],
                           [ps2d])
                    dst, dd = ((qs4, qs4_d), (ks4, ks4_d), (vT4, vT4_d))[ct // 4]
                    A("act", lambda: ACT(out=dst[:, ct % 4, :], in_=ps2[:, :], func=AF.Silu), reads=[ps2d], writes=[dd])
                    yield
                for which in range(2):
                    src, sd = (qs4, qs4_d) if which == 0 else (ks4, ks4_d)
                    dst, dd = (qT4, qT4_d) if which == 0 else (kT4, kT4_d)
                    for h in range(4):
                        A("act", lambda: ACT(out=sqb[:], in_=src[:, h, :], func=AF.Square), reads=[sd], writes=[sqb_d])
                        ps, psd = PSF()
                        mm(ps[:, :], ones_b[:], sqb[:], True, True, [sqb_d, cdep], [psd])
                        A("act", lambda: ACT(out=lnb[:], in_=ps[:, :], func=AF.Ln, bias=EPS), reads=[psd], writes=[lnb_d])
                        A("act", lambda: ACT(out=rinv[:], in_=lnb[:], func=AF.Exp, scale=-0.5,
                                             bias=(LNQ if which == 0 else 0.0)), reads=[lnb_d], writes=[rinv_d])
                        A("dve", lambda: V.tensor_tensor(out=dst[:, h, :], in0=src[:, h, :], in1=rinv[:], op=ALU.mult),
                          reads=[sd, rinv_d], writes=[dd])
                        yield
                ps, psd = PSF()
                for b in range(4):
                    tt = tg * 4 + b
                    for c in range(8):
                        mm(ps[:, b * 8:(b + 1) * 8], hT[:, c, tt * 128:(tt + 1) * 128], wdn[:, c, 2048:2056], c == 0, c == 7,
                           [hT_all[tt], wdn_dep], [psd])
                A("dve", lambda: V.tensor_copy(out=ba[:], in_=ps[:, 0:32].rearrange("p (b e) -> p b e", b=4)), reads=[psd],
                  writes=[ba_d])
                A("act", lambda: ACT(out=sm["beta"][:], in_=ba[:, :, 0:4], func=AF.Sigmoid), reads=[ba_d],
                  writes=[sm_d["beta"]])
                A("dve", lambda: V.tensor_scalar(out=sm["nbeta"][:], in0=sm["beta"][:], scalar1=-1.0, scalar2=None,
                                                 op0=ALU.mult), reads=[sm_d["beta"]], writes=[sm_d["nbeta"]])
                A("dve", lambda: V.tensor_tensor(out=sm["x1"][:], in0=ba[:, :, 4:8], in1=dtb16[:], op=ALU.add),
                  reads=[ba_d, c16_dep], writes=[sm_d["x1"]])
                A("dve", lambda: V.tensor_scalar(out=sm["mx"][:], in0=sm["x1"][:], scalar1=-1.0, scalar2=None, op0=ALU.mult),
                  reads=[sm_d["x1"]], writes=[sm_d["mx"]])
                A("dve", lambda: V.tensor_tensor(out=sm["ax"][:], in0=sm["x1"][:], in1=sm["mx"][:], op=ALU.max),
                  reads=[sm_d["x1"], sm_d["mx"]], writes=[sm_d["ax"]])
                A("act", lambda: ACT(out=sm["e"][:], in_=sm["ax"][:], func=AF.Exp, scale=-1.0), reads=[sm_d["ax"]],
                  writes=[sm_d["e"]])
                A("act", lambda: ACT(out=sm["l"][:], in_=sm["e"][:], func=AF.Ln, bias=1.0), reads=[sm_d["e"]],
                  writes=[sm_d["l"]])
                A("dve", lambda: V.tensor_scalar(out=sm["mx"][:], in0=sm["x1"][:], scalar1=0.0, scalar2=None, op0=ALU.max),
                  reads=[sm_d["x1"]], writes=[sm_d["mx"]])
                A("dve", lambda: V.tensor_tensor(out=sm["sp"][:], in0=sm["mx"][:], in1=sm["l"][:], op=ALU.add),
                  reads=[sm_d["mx"], sm_d["l"]], writes=[sm_d["sp"]])
                A("dve", lambda: V.tensor_tensor(out=sm["g"][:], in0=sm["sp"][:], in1=nA16[:], op=ALU.mult),
                  reads=[sm_d["sp"], c16_dep], writes=[sm_d["g"]])
                ps, psd = PSF()
                g16 = sm["g"][:].rearrange("p b h -> p (b h)")
                mm(ps[:, 0:16], tri[:], g16, True, True, [sm_d["g"], cdep], [psd])
                mm(ps[:, 16:32], ones_f[:], g16, True, True, [sm_d["g"], cdep], [psd])
                v16 = lambda n: sm[n][:].rearrange("p b h -> p (b h)")
                A("dve", lambda: V.tensor_copy(out=v16("gc"), in_=ps[:, 0:16]), reads=[psd], writes=[sm_d["gc"]])
                A("act", lambda: ACT(out=v16("egc"), in_=ps[:, 0:16], func=AF.Exp), reads=[psd], writes=[sm_d["egc"]])
                A("dve", lambda: V.tensor_scalar(out=v16("ngc"), in0=v16("gc"), scalar1=-1.0, scalar2=None, op0=ALU.mult),
                  reads=[sm_d["gc"]], writes=[sm_d["ngc"]])
                A("dve", lambda: V.tensor_tensor(out=v16("ebg"), in0=v16("egc"), in1=v16("beta"), op=ALU.mult),
                  reads=[sm_d["egc"], sm_d["beta"]], writes=[sm_d["ebg"]])
                A("dve", lambda: V.tensor_tensor(out=v16("dlt"), in0=ps[:, 16:32], in1=v16("gc"), op=ALU.subtract),
                  reads=[psd, sm_d["gc"]], writes=[sm_d["dlt"]])
                A("act", lambda: ACT(out=v16("etail"), in_=v16("dlt"), func=AF.Exp), reads=[sm_d["dlt"]],
                  writes=[sm_d["etail"]])
                A("act", lambda: ACT(out=v16("gl"), in_=ps[:, 16:32], func=AF.Exp), reads=[psd], writes=[sm_d["gl"]])

                yield

            def gen_pre(b):
                P = PRE[b % 2]
                bs = slice(b * 128, (b + 1) * 128)
                sc = lambda n, h: sm[n][:, b, h:h + 1]
                for h in range(4):
                    A("pool", lambda: G.tensor_scalar(out=diag4[:, h, :], in0=ones_f[:], scalar1=sc("gc", h),
                                                      scalar2=0.0, op0=ALU.mult, op1=ALU.add),
                      reads=[cdep, sm_d["gc"]], writes=[diag4_d])
                pR, pRd = PSF2()
                for h in range(4):
                    hs = slice(h * 128, (h + 1) * 128)
                    A("pe", lambda: nc.tensor.transpose(pR[:, hs], diag4[:, h, :], ident_f[:]), reads=[diag4_d, cdep],
                      writes=[pRd])
                A("act", lambda: ACT(out=f4(Eg4), in_=pR[:, :], func=AF.Exp), reads=[pRd], writes=[Eg4_d])
                A("dve", lambda: V.tensor_tensor(out=f4(rgmu), in0=pR[:, :], in1=f4(mu4), op=ALU.add),
                  reads=[pRd, mk4_d], writes=[rgmu_d])
                A("dve", lambda: V.scalar_tensor_tensor(out=f4(rgml), in0=pR[:, :], scalar=-1.0, in1=f4(ml4),
                                                        op0=ALU.mult, op1=ALU.add),
                  reads=[pRd, mk4_d], writes=[rgml_d])
                for h in range(4):
                    A("act", lambda: ACT(out=DT4[:, h, :], in_=rgmu[:, h, :], func=AF.Exp, bias=sc("ngc", h)),
                      reads=[rgmu_d, sm_d["ngc"]], writes=[DT4_d])
                    A("act", lambda: ACT(out=D4[:, h, :], in_=rgml[:, h, :], func=AF.Exp, bias=sc("gc", h)),
                      reads=[rgml_d, sm_d["gc"]], writes=[D4_d])
                yield
                pG, pGd = PSF2()
                for h in range(4):
                    hs = slice(h * 128, (h + 1) * 128)
                    mm(pG[:, hs], kT4[:, h, bs], kT4[:, h, bs], True, True, [kT4_d], [pGd])
                for h in range(4):
                    hs = slice(h * 128, (h + 1) * 128)
                    A("dve", lambda: V.scalar_tensor_tensor(out=P["N4"][:, h, :], in0=pG[:, hs], scalar=sc("nbeta", h),
                                                            in1=D4[:, h, :], op0=ALU.mult, op1=ALU.mult),
                      reads=[pGd, D4_d, sm_d["nbeta"]], writes=[P["N4d"]])
                yield
                pQ, pQd = PSF2()
                for h in range(4):
                    hs = slice(h * 128, (h + 1) * 128)
                    mm(pQ[:, hs], kT4[:, h, bs], qT4[:, h, bs], True, True, [kT4_d, qT4_d], [pQd])
                A("dve", lambda: V.tensor_tensor(out=P["ka4"][:, :, 128:256],
                                                 in0=pQ[:, :].rearrange("p (h w) -> p h w", h=4), in1=DT4[:],
                                                 op=ALU.mult), reads=[pQd, DT4_d], writes=[P["ka4d"]])
                A("pool", lambda: G.tensor_tensor(out=P["qd4"][:], in0=qT4[:, :, bs], in1=Eg4[:], op=ALU.mult),
                  reads=[qT4_d, Eg4_d], writes=[P["qd4d"]])
                yield
                pT1, pT1d = PSB()
                for h in range(4):
                    hs = slice(h * 128, (h + 1) * 128)
                    A("pe", lambda: nc.tensor.transpose(pT1[:, hs], P["N4"][:, h, :], ident_b[:]),
                      reads=[P["N4d"], cdep], writes=[pT1d])
                A("act", lambda: nc.scalar.copy(out=f4(P["Y4"]), in_=pT1[:, 0:512]), reads=[pT1d], writes=[P["Y4d"]])
                yield
                pT2, pT2d = PSB()
                for h in range(4):
                    hs = slice(h * 128, (h + 1) * 128)
                    A("pe", lambda: nc.tensor.transpose(pT2[:, hs], vT4[:, h, bs], ident_b[:]), reads=[vT4_d, cdep],
                      writes=[pT2d])
                    A("pe", lambda: nc.tensor.transpose(pT2[:, 512 + h * 128:512 + (h + 1) * 128], kT4[:, h, bs],
                                                        ident_b[:]), reads=[kT4_d, cdep], writes=[pT2d])
                for h in range(4):
                    hs = slice(h * 128, (h + 1) * 128)
                    ks_ = slice(512 + h * 128, 512 + (h + 1) * 128)
                    A("dve", lambda: V.tensor_scalar(out=P["vbk4"][:, h, 0:128], in0=pT2[:, hs], scalar1=sc("beta", h),
                                                     scalar2=None, op0=ALU.mult),
                      reads=[pT2d, sm_d["beta"]], writes=[P["vbk4d"]])
                    A("dve", lambda: V.tensor_scalar(out=P["vbk4"][:, h, 128:256], in0=pT2[:, ks_],
                                                     scalar1=sc("ebg", h), scalar2=None, op0=ALU.mult),
                      reads=[pT2d, sm_d["ebg"]], writes=[P["vbk4d"]])
                    A("dve", lambda: V.tensor_scalar(out=P["ka4"][:, h, 0:128], in0=pT2[:, ks_], scalar1=sc("etail", h),
                                                     scalar2=None, op0=ALU.mult),
                      reads=[pT2d, sm_d["etail"]], writes=[P["ka4d"]])
                yield

            def gen_inv(b, h):
                P = PRE[b % 2]
                bs = slice(b * 128, (b + 1) * 128)
                sc = lambda n: sm[n][:, b, h:h + 1]
                bank, bankd = psf[h], psf_dep[h]
                X, Xd = P["N4"][:, h, :], P["N4d"]
                Y, Yd = P["Y4"][:, h, :], P["Y4d"]
                YR = None
                R = None
                for j in range(1, 7):
                    nt, ntd = XYR[h][j % 2]
                    ev = "act" if ((h + j) % 2 == 0 and os.environ.get("K_EVDVE", "0") != "1") else "dve"
                    mm(bank[:, 0:128], Y, X, True, True, [Xd, Yd], [bankd])
                    if j == 1:
                        mm(bank[:, 128:256], X, Y, True, True, [Xd, Yd], [bankd])
                        A("pool", lambda: G.tensor_tensor(out=nt[:, 256:384], in0=ident_b[:], in1=Y, op=ALU.add),
                          reads=[cdep, Yd], writes=[ntd])
                        w = 256
                    elif j < 6:
                        mm(bank[:, 128:384], X, YR, True, False, [Xd], [bankd])
                        mm(bank[:, 256:384], ident_b[:], R, False, True, [cdep, Xd], [bankd])
                        w = 384
                    else:
                        mm(bank[:, 128:256], X, R, True, False, [Xd], [bankd])
                        mm(bank[:, 128:256], ident_b[:], R, False, True, [cdep, Xd], [bankd])
                        w = 256
                    if ev == "act":
                        A("act", lambda: nc.scalar.copy(out=nt[:, 0:w], in_=bank[:, 0:w]), reads=[bankd], writes=[ntd])
                    else:
                        A("dve", lambda: V.tensor_copy(out=nt[:, 0:w], in_=bank[:, 0:w]), reads=[bankd], writes=[ntd])
                    X, Xd = nt[:, 0:128], ntd
                    if j < 6:
                        Y, Yd = nt[:, 128:256], ntd
                        YR = nt[:, 128:384]
                        R = nt[:, 256:384]
                    else:
                        R = nt[:, 128:256]
                    yield
                mm(bank[:, 0:128], X, R, True, False, [Xd], [bankd])
                mm(bank[:, 0:128], ident_b[:], R, False, True, [cdep, Xd], [bankd])
                r6, r6d = R6t[h]
                A("dve" if h % 2 == 0 else "act",
                  (lambda: V.tensor_copy(out=r6[:], in_=bank[:, 0:128])) if h % 2 == 0 else
                  (lambda: nc.scalar.copy(out=r6[:], in_=bank[:, 0:128])), reads=[bankd], writes=[r6d])
                yield
                mm(bank[:, 0:256], r6[:], P["vbk4"][:, h, :], True, True, [r6d, P["vbk4d"]], [bankd])
                A("act" if h % 2 == 0 else "dve",
                  (lambda: nc.scalar.copy(out=UW4[:, h, :], in_=bank[:, 0:256])) if h % 2 == 0 else
                  (lambda: V.tensor_copy(out=UW4[:, h, :], in_=bank[:, 0:256])), reads=[bankd], writes=[UWd[h]])
                yield
                mm(bank[:, 0:256], UW4[:, h, 128:256], P["ka4"][:, h, :], True, True, [UWd[h], P["ka4d"]], [bankd])
                A("dve", lambda: V.scalar_tensor_tensor(out=Mn4[:, h, :], in0=ident_f[:], scalar=sc("gl"),
                                                        in1=bank[:, 0:128], op0=ALU.mult, op1=ALU.subtract),
                  reads=[bankd, cdep, sm_d["gl"]], writes=[Mnd[h]])
                A("dve", lambda: V.tensor_tensor(out=Qe4[:, h, :], in0=P["qd4"][:, h, :], in1=bank[:, 128:256],
                                                 op=ALU.subtract), reads=[P["qd4d"], bankd], writes=[Qed[h]])
                yield

            def emit_seq(b):
                P = PRE[b % 2]
                bs = slice(b * 128, (b + 1) * 128)
                pO, pOd = PSF2()
                pS, pSd = PSF2()
                for h in range(4):
                    hs = slice(h * 128, (h + 1) * 128)
                    mm(pO[:, hs], Sb[:, h, :], Qe4[:, h, :], True, False, [S_dep, Qed[h]], [pOd])
                    mm(pO[:, hs], UW4[:, h, 0:128], P["ka4"][:, h, 128:256], False, True, [UWd[h], P["ka4d"]], [pOd])
                for h in range(4):
                    hs = slice(h * 128, (h + 1) * 128)
                    mm(pS[:, hs], Mn4[:, h, :], Sb[:, h, :], True, False, [Mnd[h], S_dep], [pSd])
                    mm(pS[:, hs], P["ka4"][:, h, 0:128], UW4[:, h, 0:128], False, True, [P["ka4d"], UWd[h]], [pSd])
                A("act", lambda: nc.scalar.copy(out=oT4[:, :, bs], in_=pO[:, :].rearrange("p (h w) -> p h w", h=4)),
                  reads=[pOd], writes=[oT4_d])
                A("dve", lambda: V.tensor_copy(out=f4(Sb), in_=pS[:, :]), reads=[pSd], writes=[S_dep])

            def gen_g5():
                for h in range(4):
                    ps, psd = PSF2()
                    for c in range(8):
                        mm(ps[:, :], wdn[:, c, (12 + h) * 128:(13 + h) * 128], hT[:, c, ts_], c == 0, c == 7,
                           hdeps + [wdn_dep], [psd])
                    A("act", lambda: ACT(out=zsb[:], in_=ps[:, :], func=AF.Silu), reads=[psd], writes=[zsb_d])
                    A("act", lambda: ACT(out=sqb2[:], in_=oT4[:, h, :], func=AF.Square), reads=[oT4_d], writes=[sqb2_d])
                    ps, psd = PSF2()
                    mm(ps[:, :], ones_b[:], sqb2[:], True, True, [sqb2_d, cdep], [psd])
                    A("act", lambda: ACT(out=lnb2[:], in_=ps[:, :], func=AF.Ln, bias=EPS, scale=1.0 / 128), reads=[psd],
                      writes=[lnb2_d])
                    A("act", lambda: ACT(out=rinvf[:], in_=lnb2[:], func=AF.Exp, scale=-0.5), reads=[lnb2_d],
                      writes=[rinvf_d])
                    yield
                    A("pool", lambda: G.tensor_tensor(out=rinvf[:], in0=rinvf[:], in1=zsb[:], op=ALU.mult),
                      reads=[rinvf_d, zsb_d], writes=[rinvf_d])
                    yb = yi[0] % 2
                    yi[0] += 1
                    A("dve", lambda: V.scalar_tensor_tensor(out=yst[yb][:], in0=oT4[:, h, :], scalar=dnw[:, 0:1],
                                                            in1=rinvf[:], op0=ALU.mult, op1=ALU.mult),
                      reads=[oT4_d, rinvf_d, cdep], writes=[yst_d[yb]])
                    T.dma("sp", mix_d[h * 128:(h + 1) * 128, ts_], yst[yb][:], reads=[yst_d[yb]], writes=[mix_dep],
                          chan="ydn%d" % yb, accumulate=True)
                    yield

            return gen_front, gen_pre, gen_inv, emit_seq, gen_g5

        def step_all(gens):
            nxt = []
            for g_ in gens:
                try:
                    next(g_)
                    nxt.append(g_)
                    filler(FILL_DN)
                except StopIteration:
                    pass
            return nxt

        def drain(gens):
            while gens:
                gens = step_all(gens)

        fns = [group_fns(tg) for tg in range(8)]
        drain([fns[0][0]()])
        drain([fns[0][1](0)])
        g5_bg = []
        for tg in range(8):
            gen_front, gen_pre, gen_inv, emit_seq, gen_g5 = fns[tg]
            fr_bg = [fns[tg + 1][0]()] if tg + 1 < 8 else []
            if os.environ.get("K_NOBG", "0") == "1":
                drain(g5_bg)
                g5_bg = []
            for b in range(4):
                gl_ = [gen_inv(b, h) for h in range(4)]
                if b + 1 < 4:
                    gl_.append(gen_pre(b + 1))
                while gl_:
                    gl_ = step_all(gl_)
                    if os.environ.get("K_NOBG", "0") != "1":
                        if os.environ.get("K_NOFR", "0") != "1":
                            fr_bg = step_all(fr_bg)
                        if os.environ.get("K_NOG5", "0") != "1":
                            g5_bg = step_all(g5_bg)
                if b == 0:
                    drain(g5_bg)
                    g5_bg = []
                emit_seq(b)
            drain(fr_bg)
            g5_bg = [gen_g5()]
            if tg + 1 < 8:
                drain([fns[tg + 1][1](0)])
        drain(g5_bg)
        T.barrier()


_CACHE = {}


def kernel(**inputs):
    if "nc" not in _CACHE:
        _CACHE["nc"] = build()
    nc = _CACHE["nc"]
    consts = host_constants()
    x = np.ascontiguousarray(np.asarray(inputs["x"], dtype=np.float32))
    shared = {
        "norm_w": np.ascontiguousarray(np.asarray(inputs["norm_w"], np.float32)[0]),
        "w_in": np.ascontiguousarray(np.asarray(inputs["w_in"], np.float32)[0]),
        "conv_w": np.ascontiguousarray(np.asarray(inputs["conv_w"], np.float32)[0]),
        "a_log": np.ascontiguousarray(np.asarray(inputs["a_log"], np.float32)[0]),
        "dt_bias": np.ascontiguousarray(np.asarray(inputs["dt_bias"], np.float32)[0]),
        "dn_norm_w": np.ascontiguousarray(np.asarray(inputs["dn_norm_w"], np.float32)[0]),
        "q_norm_w": np.ascontiguousarray(np.asarray(inputs["q_norm_w"], np.float32)[0]),
        "k_norm_w": np.ascontiguousarray(np.asarray(inputs["k_norm_w"], np.float32)[0]),
        "rel_bias": np.ascontiguousarray(np.asarray(inputs["rel_bias"], np.float32)),
        "w_out": np.ascontiguousarray(np.asarray(inputs["w_out"], np.float32)[0]),
    }
    shared.update(consts)
    in_maps = []
    for b in range(8):
        m = dict(shared)
        m["x"] = x[b]
        in_maps.append(m)
    res = run_bass_kernel_spmd(nc, in_maps, core_ids=list(range(8)))
    return np.stack([np.asarray(r["out"], dtype=np.float32).reshape(S, D) for r in res.results], axis=0)
```

```python
import math
import os
from contextlib import ExitStack
import numpy as np
import ml_dtypes
import concourse.bass as bass
import concourse.mybir as mybir
from concourse.bass_utils import run_bass_kernel_spmd

F32 = mybir.dt.float32
BF16 = mybir.dt.bfloat16
AF = mybir.ActivationFunctionType
ALU = mybir.AluOpType

S = 4096
D = 1024
NT = 32
D_IN = 4104
EPS = 1e-6
NEG = -30000.0
PATTERNS = ((128, 1), (512, 4), (2048, 16))
N_BUCKETS = 32
MAX_DISTANCE = 2048


class Dep:
    __slots__ = ("w", "r", "name", "excl")

    def __init__(self, name="", excl=False):
        self.w = []
        self.r = []
        self.name = name
        self.excl = excl


class Tracker:
    def __init__(self, nc, es):
        self.nc = nc
        self.eng = {"pe": nc.tensor, "act": nc.scalar, "dve": nc.vector, "pool": nc.gpsimd, "sp": nc.sync}
        self.sem = {}
        self.cnt = {}
        self.known = {e: {} for e in self.eng}
        self.es = es
        for e in ("pe", "act", "dve", "pool"):
            self.sem[e] = es.enter_context(nc.semaphore("sem_" + e))
            self.cnt[e] = 0
        self.chan = {}
        self.semobj = {}
        for e in ("pe", "act", "dve", "pool"):
            self.semobj[id(self.sem[e])] = self.sem[e]

    def _chan(self, name):
        if name not in self.chan:
            s = self.es.enter_context(self.nc.semaphore("ch_" + name))
            self.chan[name] = [s, 0]
        return self.chan[name]

    def _collect(self, eng, reads, writes):
        need = {}

        def add(tok, war):
            s, v, e = tok
            if e == eng and eng == "pe":
                return
            k = id(s)
            if k not in need or need[k][1] < v:
                need[k] = (s, v)

        for d in reads:
            for t in d.w:
                add(t, False)
            if d.excl:
                for t in d.r:
                    if t[2] != eng:
                        add(t, False)
        for d in writes:
            for t in d.w:
                add(t, False)
            for t in d.r:
                add(t, True)
        return need

    def _emit_waits(self, eng, need):
        kn = self.known[eng]
        for k, (s, v) in need.items():
            if kn.get(k, 0) < v:
                self.eng[eng].wait_ge(s, v)
                kn[k] = v

    def _record(self, tok, reads, writes, accumulate=False):
        for d in writes:
            if accumulate:
                d.w.append(tok)
            else:
                d.w = [tok]
                d.r = []
        for d in reads:
            d.r = [t for t in d.r if not (t[2] == tok[2] and t[2] != "dma")] + [tok]

    def op(self, eng, fn, reads=(), writes=()):
        need = self._collect(eng, reads, writes)
        self._emit_waits(eng, need)
        inst = fn()
        self.cnt[eng] += 1
        inst.then_inc(self.sem[eng], 1)
        tok = (self.sem[eng], self.cnt[eng], eng)
        self._record(tok, reads, writes)
        return inst

    def dma(self, q, out, in_, reads=(), writes=(), chan="d", accumulate=False):
        need = self._collect(q, reads, writes)
        self._emit_waits(q, need)
        ch = self._chan(chan + "_" + q)
        inst = self.eng[q].dma_start(out=out, in_=in_)
        ch[1] += 16
        inst.then_inc(ch[0], 16)
        tok = (ch[0], ch[1], "dma")
        self._record(tok, reads, writes, accumulate=accumulate)
        return inst

    def wait_all(self, eng, deps):
        need = self._collect(eng, deps, ())
        self._emit_waits(eng, need)

    def barrier(self):
        for e in ("pe", "act", "dve", "pool", "sp"):
            need = {}
            for f in ("pe", "act", "dve", "pool"):
                if f != e and self.cnt[f] > 0:
                    need[id(self.sem[f])] = (self.sem[f], self.cnt[f])
            for name, (s, v) in self.chan.items():
                if v > 0:
                    need[id(s)] = (s, v)
            self._emit_waits(e, need)


def t5_bucket(dist):
    max_exact = N_BUCKETS // 2
    d = np.maximum(dist, 1).astype(np.float64)
    large = max_exact + (np.log(d / max_exact) / math.log(MAX_DISTANCE / max_exact)
                         * (N_BUCKETS - max_exact)).astype(np.int32)
    large = np.minimum(large, N_BUCKETS - 1)
    return np.where(dist < max_exact, dist, large).astype(np.int32)


def host_constants():
    c = {}
    eye = np.eye(128, dtype=np.float32)
    c["c_ident_f"] = eye
    c["c_ident_b"] = eye.astype(ml_dtypes.bfloat16)
    k = np.arange(128)
    c["c_tri"] = (k[:, None] <= k[None, :]).astype(np.float32)
    c["c_ones_f"] = np.ones((128, 128), np.float32)
    c["c_nones_f"] = -np.ones((128, 128), np.float32)
    c["c_ones_b"] = np.ones((128, 128), ml_dtypes.bfloat16)
    blk = np.zeros((128, 128), np.float32)
    blk[:64, :64] = 1.0 / 64
    blk[64:, 64:] = 1.0 / 64
    c["c_blk64"] = blk.astype(ml_dtypes.bfloat16)
    c["c_mu"] = np.where(k[None, :] >= k[:, None], 0.0, NEG).astype(ml_dtypes.bfloat16)
    c["c_ml"] = np.where(k[:, None] > k[None, :], 0.0, NEG).astype(ml_dtypes.bfloat16)
    oh = np.zeros((3, 32, 384), np.float32)
    for pi, (win, r) in enumerate(PATTERNS):
        s = np.arange(0, 129)
        b = t5_bucket(s * r)
        for si, bi in zip(s, b):
            oh[pi, bi, si + 127] = 1.0
    c["c_onehot"] = oh
    selA = np.zeros((128, 128), np.float32)
    selA[64, 0:64] = 1.0
    selB = np.zeros((128, 128), np.float32)
    selB[0, 64:128] = 1.0
    c["c_selA"] = selA.astype(ml_dtypes.bfloat16)
    c["c_selB"] = selB.astype(ml_dtypes.bfloat16)
    return c


CONST_SHAPES = {
    "c_ident_f": ([128, 128], F32), "c_ident_b": ([128, 128], BF16), "c_tri": ([128, 128], F32),
    "c_ones_f": ([128, 128], F32), "c_nones_f": ([128, 128], F32), "c_ones_b": ([128, 128], BF16),
    "c_blk64": ([128, 128], BF16), "c_mu": ([128, 128], BF16), "c_ml": ([128, 128], BF16),
    "c_onehot": ([3, 32, 384], F32), "c_selA": ([128, 128], BF16), "c_selB": ([128, 128], BF16),
}


def build(debug=False, do_dn=True, do_att=True):
    nc = bass.Bass("TRN2", target_bir_lowering=False)
    es = ExitStack()
    T = Tracker(nc, es)
    es.enter_context(nc.allow_non_contiguous_dma(reason="small params / strided layouts"))
    dt = nc.dram_tensor
    x_d = dt("x", [S, D], F32, kind="ExternalInput").ap()
    normw_d = dt("norm_w", [D], F32, kind="ExternalInput").ap()
    win_d = dt("w_in", [D, D_IN], F32, kind="ExternalInput").ap()
    convw_d = dt("conv_w", [4, 1536], F32, kind="ExternalInput").ap()
    alog_d = dt("a_log", [4], F32, kind="ExternalInput").ap()
    dtb_d = dt("dt_bias", [4], F32, kind="ExternalInput").ap()
    dnw_d = dt("dn_norm_w", [128], F32, kind="ExternalInput").ap()
    qnw_d = dt("q_norm_w", [64], F32, kind="ExternalInput").ap()
    knw_d = dt("k_norm_w", [64], F32, kind="ExternalInput").ap()
    relb_d = dt("rel_bias", [8, 32], F32, kind="ExternalInput").ap()
    wout_d = dt("w_out", [D, D], F32, kind="ExternalInput").ap()
    cst_d = {n: dt(n, sh, ty, kind="ExternalInput").ap() for n, (sh, ty) in CONST_SHAPES.items()}
    out_d = dt("out", [S, D], F32, kind="ExternalOutput").ap()
    vscr_d = dt("vscr", [S, 512], BF16, kind="Internal").ap()
    mix_d = dt("mixscr", [D, S], BF16, kind="Internal").ap()
    gscr_d = dt("gscr", [24, 128 * 384], BF16, kind="Internal").ap()
    dbg_d = {}
    if debug:
        dbg_d["mix"] = dt("dbg_mix", [D, S], BF16, kind="ExternalOutput").ap()

    uniq = [0]

    def sb(name, shape, ty, stack=None):
        uniq[0] += 1
        return (stack or es).enter_context(nc.sbuf_tensor("%s_%d" % (name, uniq[0]), shape, ty))

    A = T.op
    ACT = lambda **kw: nc.scalar.activation(**kw)

    psbig = [es.enter_context(nc.psum_tensor("psbig%d" % i, [128, 1024], F32)) for i in range(2)]
    psf = [psbig[0][:, 0:512], psbig[0][:, 512:1024], psbig[1][:, 0:512], psbig[1][:, 512:1024]]
    psf += [es.enter_context(nc.psum_tensor("psf%d" % i, [128, 512], F32))[:, :] for i in range(2)]
    psb = [es.enter_context(nc.psum_tensor("psb%d" % i, [128, 1024], BF16)) for i in range(2)]
    psf_dep = [Dep("psf%d" % i, excl=True) for i in range(6)]
    psb_dep = [Dep("psb%d" % i, excl=True) for i in range(2)]
    rot = {"f": 0, "b": 0}

    def PSF():
        i = rot["f"]
        rot["f"] = (i + 1) % 6
        return psf[i], psf_dep[i]

    def PSB():
        i = rot["b"]
        rot["b"] = (i + 1) % 2
        return psb[i], psb_dep[i]

    fill_ap = psb[1][:, :].bitcast(F32)

    def filler(n):
        for _ in range(n):
            nc.tensor.matmul(fill_ap[:, 0:128], cst["c_ident_b"][:], cst["c_ident_b"][:], start=True, stop=True)

    cst = {}
    cdep = Dep("consts")
    for n, (sh, ty) in CONST_SHAPES.items():
        if n == "c_onehot":
            continue
        cst[n] = sb("s_" + n, sh, ty)
        T.dma("sp", cst[n][:], cst_d[n][:, :], writes=[cdep], chan="const", accumulate=True)
    ident_f, ident_b = cst["c_ident_f"], cst["c_ident_b"]
    qnw = sb("qnw", [128, 1], F32)
    knw = sb("knw", [128, 1], F32)
    for hh in range(2):
        T.dma("sp", qnw[hh * 64:(hh + 1) * 64, :], qnw_d.rearrange("(p o) -> p o", o=1), writes=[cdep],
              chan="const", accumulate=True)
        T.dma("sp", knw[hh * 64:(hh + 1) * 64, :], knw_d.rearrange("(p o) -> p o", o=1), writes=[cdep],
              chan="const", accumulate=True)
    dnw = sb("dnw", [128, 1], F32)
    T.dma("sp", dnw[:], dnw_d.rearrange("(p o) -> p o", o=1), writes=[cdep], chan="const", accumulate=True)
    relbT = sb("relbT", [32, 8], F32)
    T.dma("sp", relbT[:], relb_d.rearrange("h b -> b h"), writes=[cdep], chan="const", accumulate=True)
    convw = sb("convw", [128, 12, 4], F32)
    for i in range(4):
        T.dma("sp", convw[:, :, i], convw_d[i, :].rearrange("(t p) -> p t", p=128), writes=[cdep], chan="const",
              accumulate=True)
    alog_bc = sb("alog_bc", [128, 4], F32)
    dtb_bc = sb("dtb_bc", [128, 4], F32)
    T.dma("sp", alog_bc[:], alog_d.partition_broadcast(128), writes=[cdep], chan="const", accumulate=True)
    T.dma("sp", dtb_bc[:], dtb_d.partition_broadcast(128), writes=[cdep], chan="const", accumulate=True)
    qnw_s = sb("qnw_s", [128, 1], F32)
    qnw_dep = Dep()
    A("dve", lambda: nc.vector.tensor_scalar(out=qnw_s[:], in0=qnw[:], scalar1=0.125, scalar2=None, op0=ALU.mult),
      reads=[cdep], writes=[qnw_dep])

    qkg_d = dt("qkgscr", [3, 512, S], BF16, kind="Internal").ap()
    mix_dep = Dep("mix")
    vscr_dep = Dep("vscr")
    qkg_dep = Dep("qkg")
    win_v = win_d.rearrange("(c p) n -> p c n", p=128)

    wl_i = [0]

    def make_loader(stack):
        def load_w(dst, dst_dep, c0, n, src_v=None, scale=True, q="pool"):
            src_ = (src_v if src_v is not None else win_v)
            wl_i[0] += 1
            for c in range(8):
                T.dma("pool", dst[:, c, 0:n], src_[:, c, c0:c0 + n], writes=[dst_dep], chan="wl%d" % wl_i[0],
                      accumulate=True)
        return load_w

    def zero_mix(r0, r1):
        with ExitStack() as z:
            zt = sb("zt", [128, S], BF16, z)
            zd = Dep()
            A("pool", lambda: nc.gpsimd.memset(zt[:], 0.0), writes=[zd])
            for r in range(r0, r1, 128):
                T.dma("sp", mix_d[r:r + 128, :], zt[:], reads=[zd], writes=[mix_dep], chan="zmix", accumulate=True)
            T.barrier()

    with ExitStack() as pH:
        hT = sb("hT", [128, 8, S], BF16, pH)
        hT_dep = [Dep("hT%d" % i) for i in range(NT)]
        load_w0 = make_loader(pH)
        wdn = sb("wdn", [128, 8, 2056], BF16, pH)
        wdn_dep = Dep("wdn")
        pW = ExitStack()
        wb = [sb("wb%d" % i, [128, 8, 512], BF16, pW) for i in range(2)]
        wb_dep = [Dep() for _ in range(2)]
        if do_att:
            load_w0(wb[0], wb_dep[0], 2056, 512)
            load_w0(wb[1], wb_dep[1], 2056 + 512, 512)
        if do_dn:
            for g4 in range(4):
                load_w0(wdn[:, :, g4 * 512:(g4 + 1) * 512], wdn_dep, g4 * 512, 512)
            load_w0(wdn[:, :, 2048:2056], wdn_dep, 2048, 8)
        g_dep = Dep("gscr")
        if do_att:
            onehot = sb("onehot", [32, 3, 384], F32, pW)
            T.dma("sp", onehot[:], cst_d["c_onehot"].rearrange("p b i -> b p i"), writes=[cdep], chan="const_oh",
                  accumulate=True)
            expb = sb("expb", [32, 8], F32, pW)
            expb_dep = Dep()
            A("act", lambda: ACT(out=expb[:], in_=relbT[:], func=AF.Exp), reads=[cdep], writes=[expb_dep])
            ebc = sb("ebc", [32, 8, 128], F32, pW)
            ebc_dep = Dep()
            for h in range(8):
                A("dve", lambda h=h: nc.vector.tensor_scalar(out=ebc[:, h, :], in0=cst["c_ones_f"][0:32, :],
                                                             scalar1=expb[:, h:h + 1], scalar2=None, op0=ALU.mult),
                  reads=[expb_dep, cdep], writes=[ebc_dep])
            rowrep = [sb("rowrep%d" % i, [128, 384], BF16, pW) for i in range(2)]
            rr_dep = [Dep() for _ in range(2)]
            for pi in range(3):
                for h in range(8):
                    idx = pi * 8 + h
                    ps, psd = PSF()
                    A("pe", lambda: nc.tensor.matmul(ps[:, 0:384], ebc[:, h, :], onehot[:, pi, :], start=True, stop=True),
                      reads=[ebc_dep, cdep], writes=[psd])
                    rb = idx % 2
                    A("dve", lambda: nc.vector.tensor_copy(out=rowrep[rb][:], in_=ps[:, 0:384]), reads=[psd],
                      writes=[rr_dep[rb]])
                    T.dma("sp", gscr_d[idx, :].rearrange("(p i) -> p i", i=384), rowrep[rb][:], reads=[rr_dep[rb]],
                          writes=[g_dep], chan="rr%d" % rb, accumulate=True)
        if True:
            p1 = pW
            normw_bc = sb("normw_bc", [128, D], F32, p1)
            T.dma("sp", normw_bc[:], normw_d.partition_broadcast(128), writes=[cdep], chan="const_nw", accumulate=True)
            xt = [sb("xt%d" % i, [128, D], F32, p1) for i in range(3)]
            xt_dep = [Dep() for _ in range(3)]
            junk = sb("junk", [128, D], BF16, p1)
            junk_dep = Dep()
            xs = [sb("xs%d" % i, [128, D], BF16, p1) for i in range(2)]
            xs_dep = [Dep() for _ in range(2)]
            junk2 = sb("junk2", [128, D], BF16, p1)
            ss = sb("ss", [128, NT], F32, p1)
            rs = sb("rs", [128, NT], F32, p1)
            ss_dep = [Dep() for _ in range(NT)]
            def p1_tile(tt):
                b3 = tt % 3
                b2 = tt % 2
                T.dma("sp", xt[b3][:], x_d[tt * 128:(tt + 1) * 128, :], writes=[xt_dep[b3]], chan="xt%d" % b3)
                A("act", lambda: ACT(out=junk[:], in_=xt[b3][:], func=AF.Square, accum_out=ss[:, tt:tt + 1]),
                  reads=[xt_dep[b3]], writes=[junk_dep, ss_dep[tt]])
                A("dve", lambda: nc.vector.tensor_scalar(out=rs[:, tt:tt + 1], in0=ss[:, tt:tt + 1], scalar1=1.0 / D,
                                                         scalar2=EPS, op0=ALU.mult, op1=ALU.add),
                  reads=[ss_dep[tt]], writes=[ss_dep[tt]])
                A("act", lambda: ACT(out=rs[:, tt:tt + 1], in_=rs[:, tt:tt + 1], func=AF.Sqrt),
                  reads=[ss_dep[tt]], writes=[ss_dep[tt]])
                A("dve", lambda: nc.vector.reciprocal(out=rs[:, tt:tt + 1], in_=rs[:, tt:tt + 1]),
                  reads=[ss_dep[tt]], writes=[ss_dep[tt]])
                A("dve", lambda: nc.vector.scalar_tensor_tensor(out=xs[b2][:], in0=xt[b3][:], scalar=rs[:, tt:tt + 1],
                                                                in1=normw_bc[:], op0=ALU.mult, op1=ALU.mult),
                  reads=[xt_dep[b3], ss_dep[tt], cdep], writes=[xs_dep[b2]])
            def p1_tileB(tt):
                b2 = tt % 2
                pt, ptd = PSB()
                for c in range(8):
                    A("pe", lambda c=c: nc.tensor.transpose(pt[:, c * 128:(c + 1) * 128],
                                                            xs[b2][:, c * 128:(c + 1) * 128], ident_b[:]),
                      reads=[xs_dep[b2], cdep], writes=[ptd])
                A("act", lambda: nc.scalar.copy(out=hT[:, :, tt * 128:(tt + 1) * 128],
                                                in_=pt[:, :].rearrange("p (c t) -> p c t", c=8)),
                  reads=[ptd], writes=[hT_dep[tt]])
        if do_att:
            start_group, proj_step, v_step = att_proj(nc, T, locals(), pW)
        p1_tile(0)
        for tg in range(8):
            for tt in range(tg * 4, tg * 4 + 4):
                if tt + 1 < NT:
                    p1_tile(tt + 1)
                p1_tileB(tt)
            if do_att and tg >= 1:
                proj_step(0, tg - 1)
        if do_att:
            proj_step(0, 7)
            for gi in (1, 2):
                start_group(gi)
                for tg in range(8):
                    proj_step(gi, tg)
            start_group(3)
            for tt in range(NT):
                v_step(tt)
        T.barrier()
        pW.close()
        if do_dn:
            phase_dn(nc, T, locals())
        T.barrier()
    if not do_dn:
        zero_mix(0, 512)
    wo = sb("wo", [128, 8, D], BF16)
    wo_dep = Dep()
    wout_v = wout_d.rearrange("(c p) n -> p c n", p=128)
    load_wo = make_loader(es)

    def issue_wo_loads():
        for g in range(2):
            load_wo(wo[:, :, g * 512:(g + 1) * 512], wo_dep, g * 512, 512, src_v=wout_v)

    if do_att:
        att_compute(nc, T, locals())
    else:
        issue_wo_loads()
        zero_mix(512, 1024)

    with ExitStack() as p5:
        mixT = sb("mixT", [128, 8, S], BF16, p5)
        mixT_dep = Dep()
        for c in range(8):
            T.dma("sp" if c % 2 == 0 else "pool", mixT[:, c, :], mix_d[c * 128:(c + 1) * 128, :], reads=[mix_dep],
                  writes=[mixT_dep], chan="mixT", accumulate=True)
            if debug:
                pass
        xr = [sb("xr%d" % i, [128, D], F32, p5) for i in range(3)]
        xr_dep = [Dep() for _ in range(3)]
        ot = [sb("ot%d" % i, [128, D], F32, p5) for i in range(2)]
        ot_dep = [Dep() for _ in range(2)]
        out_dep = Dep("out")
        for tt in range(NT):
            b3 = tt % 3
            b2 = tt % 2
            T.dma("pool", xr[b3][:], x_d[tt * 128:(tt + 1) * 128, :], writes=[xr_dep[b3]], chan="xr%d" % b3)
            for half in range(2):
                ps, psd = PSF()
                for c in range(8):
                    A("pe", lambda c=c: nc.tensor.matmul(ps[:, :], mixT[:, c, tt * 128:(tt + 1) * 128],
                                                         wo[:, c, half * 512:(half + 1) * 512], start=(c == 0),
                                                         stop=(c == 7)),
                      reads=[mixT_dep, wo_dep], writes=[psd])
                A("dve", lambda: nc.vector.tensor_tensor(out=ot[b2][:, half * 512:(half + 1) * 512], in0=ps[:, :],
                                                         in1=xr[b3][:, half * 512:(half + 1) * 512], op=ALU.add),
                  reads=[psd, xr_dep[b3]], writes=[ot_dep[b2]])
            T.dma("sp", out_d[tt * 128:(tt + 1) * 128, :], ot[b2][:], reads=[ot_dep[b2]], writes=[out_dep],
                  chan="ot%d" % b2, accumulate=True)
        if debug:
            for c in range(8):
                T.dma("sp", dbg_d["mix"][c * 128:(c + 1) * 128, :], mixT[:, c, :], reads=[mixT_dep], writes=[out_dep],
                      chan="dbgmix", accumulate=True)
        T.wait_all("sp", [out_dep])
    T.barrier()
    es.close()
    return nc


def att_proj(nc, T, E, pa):
    sb, A, ACT, PSF = E["sb"], E["A"], E["ACT"], E["PSF"]
    hT, hT_all, cst, cdep = E["hT"], E["hT_dep"], E["cst"], E["cdep"]
    vscr_d, qkg_d, vscr_dep, qkg_dep = E["vscr_d"], E["qkg_d"], E["vscr_dep"], E["qkg_dep"]
    qnw_s, qnw_dep, knw = E["qnw_s"], E["qnw_dep"], E["knw"]
    C0 = 2056
    load_w = E["make_loader"](pa)
    wb, wb_dep = E["wb"], E["wb_dep"]
    sq = [sb("sq%d" % i, [128, 512], BF16, pa) for i in range(2)]
    sq_dep = [Dep() for _ in range(2)]
    ln = [sb("ln%d" % i, [128, 512], F32, pa) for i in range(2)]
    ln_dep = [Dep() for _ in range(2)]
    vst = [sb("vst%d" % i, [128, 512], BF16, pa) for i in range(3)]
    vst_dep = [Dep() for _ in range(3)]
    k2 = [0]
    k3 = [0]
    groups = ((C0, "q"), (C0 + 512, "k"), (C0 + 1536, "g"), (C0 + 1024, "v"))

    def start_group(gi):
        if gi >= 2:
            load_w(wb[gi % 2], wb_dep[gi % 2], groups[gi][0], 512)

    def v_step(tt):
        wbi = 1
        ps, psd = PSF()
        for c in range(8):
            A("pe", lambda c=c: nc.tensor.matmul(ps[:, :], hT[:, c, tt * 128:(tt + 1) * 128], wb[wbi][:, c, :],
                                                 start=(c == 0), stop=(c == 7)),
              reads=[hT_all[tt], wb_dep[wbi]], writes=[psd])
        b = k3[0] % 3
        k3[0] += 1
        A("act", lambda: nc.scalar.copy(out=vst[b][:], in_=ps[:, :]), reads=[psd], writes=[vst_dep[b]])
        T.dma("sp", vscr_d[tt * 128:(tt + 1) * 128, :], vst[b][:], reads=[vst_dep[b]], writes=[vscr_dep],
              chan="vst%d" % b, accumulate=True)

    def proj_step(gi, tg):
        c0, kind = groups[gi]
        wbi = gi % 2
        ki = {"q": 0, "k": 1, "g": 2}[kind]
        for ct in range(4):
            ps, psd = PSF()
            for c in range(8):
                A("pe", lambda c=c: nc.tensor.matmul(ps[:, :], wb[wbi][:, c, ct * 128:(ct + 1) * 128],
                                                     hT[:, c, tg * 512:(tg + 1) * 512], start=(c == 0), stop=(c == 7)),
                  reads=hT_all[tg * 4:tg * 4 + 4] + [wb_dep[wbi]], writes=[psd])
            ob = k3[0] % 3
            k3[0] += 1
            if kind == "g":
                A("act", lambda: ACT(out=vst[ob][:], in_=ps[:, :], func=AF.Silu), reads=[psd], writes=[vst_dep[ob]])
            else:
                b = k2[0] % 2
                k2[0] += 1
                A("act", lambda: ACT(out=sq[b][:], in_=ps[:, :], func=AF.Square), reads=[psd], writes=[sq_dep[b]])
                ps2, ps2d = PSF()
                A("pe", lambda: nc.tensor.matmul(ps2[:, :], cst["c_blk64"][:], sq[b][:], start=True, stop=True),
                  reads=[sq_dep[b], cdep], writes=[ps2d])
                A("act", lambda: ACT(out=ln[b][:], in_=ps2[:, :], func=AF.Ln, bias=EPS), reads=[ps2d],
                  writes=[ln_dep[b]])
                A("act", lambda: ACT(out=ln[b][:], in_=ln[b][:], func=AF.Exp, scale=-0.5), reads=[ln_dep[b]],
                  writes=[ln_dep[b]])
                nw = qnw_s if kind == "q" else knw
                A("dve", lambda: nc.vector.scalar_tensor_tensor(out=vst[ob][:], in0=ps[:, :], scalar=nw[:, 0:1],
                                                                in1=ln[b][:], op0=ALU.mult, op1=ALU.mult),
                  reads=[psd, ln_dep[b], qnw_dep, cdep], writes=[vst_dep[ob]])
            T.dma("sp", qkg_d[ki, ct * 128:(ct + 1) * 128, tg * 512:(tg + 1) * 512], vst[ob][:],
                  reads=[vst_dep[ob]], writes=[qkg_dep], chan="vst%d" % ob, accumulate=True)

    return start_group, proj_step, v_step


def att_compute(nc, T, E):
    sb, A, ACT = E["sb"], E["A"], E["ACT"]
    psf, psf_dep = E["psf"], E["psf_dep"]
    cst, cdep = E["cst"], E["cdep"]
    vscr_d, qkg_d, vscr_dep, qkg_dep = E["vscr_d"], E["qkg_d"], E["vscr_dep"], E["qkg_dep"]
    mix_d, mix_dep, gscr_d = E["mix_d"], E["mix_dep"], E["gscr_d"]
    relbT = E["relbT"]
    cst_d = E["cst_d"]
    PSF = E["PSF"]
    with ExitStack() as pa:
        Et = sb("Et", [128, 24, 256], BF16, pa)
        Et_dep = Dep("Et")
        Et_deps = [Dep("Et%d" % i) for i in range(24)]
        g_dep = E["g_dep"]

        qkg = [sb("qkg%d" % i, [128, 3, S], BF16, pa) for i in range(2)]
        qkgs_dep = [Dep() for _ in range(2)]
        accs = [[sb("acc%d_%d" % (p_, i), [128, S], F32, pa) for i in range(2)] for p_ in range(2)]
        accs_dep = [[Dep() for _ in range(2)] for _ in range(2)]
        dhl = sb("dhl", [128, 2, S], BF16, pa)
        dhl_dep = Dep()
        slab = [sb("slab%d" % i, [128, 32, 192], BF16, pa) for i in range(2)]
        slab_dep = [Dep() for _ in range(2)]
        for i in range(2):
            A("pool", lambda i=i: nc.gpsimd.memset(slab[i][:], 0.0), writes=[slab_dep[i]])
            A("pool", lambda i=i: nc.gpsimd.memset(slab[i][:, :, 64:65], 1.0), writes=[slab_dep[i]])
        NB = 6
        import os
        DEPTH = int(os.environ.get('K_DEPTH', '4'))
        for p_ in range(2):
            for i in range(2):
                A("dve" if p_ == 0 else "pool",
                  (lambda: nc.vector.memset(accs[p_][i][:], 0.0)) if p_ == 0 else
                  (lambda: nc.gpsimd.memset(accs[p_][i][:], 0.0)), writes=[accs_dep[p_][i]])
        PREF = int(os.environ.get('K_PREF', '1'))
        pT = [sb("pT%d" % i, [128, 2, 256], BF16, pa) for i in range(NB)]
        pT_dep = [Dep() for _ in range(NB)]
        eT = [sb("eT%d" % i, [128, 2, 256], BF16, pa) for i in range(NB)]
        eT_dep = [Dep() for _ in range(NB)]
        rden = [sb("rden%d" % i, [128, 512], F32, pa) for i in range(2)]
        rden_dep = [Dep() for _ in range(2)]
        yst = [sb("yst%d" % i, [128, 512], BF16, pa) for i in range(2)]
        yst_dep = [Dep() for _ in range(2)]
        y_i = [0]
        rot_s = [0]
        rot_o = [0]

        psbig = E["psbig"]
        big_dep = [Dep(excl=True) for _ in range(2)]

        def PS_S():
            i = rot_s[0]
            rot_s[0] = (i + 1) % 2
            return psbig[i], big_dep[i]

        po_banks = [(psf[4], psf_dep[4]), (psf[5], psf_dep[5]),
                    (E["psb"][0][:, :].bitcast(F32), E["psb_dep"][0]), (E["psb"][1][:, :].bitcast(F32), E["psb_dep"][1])]

        def PS_O():
            i = rot_o[0]
            rot_o[0] = (rot_o[0] + 1) % 4
            return po_banks[i]

        def load_pair(hp):
            b = hp % 2
            for ki in range(3):
                T.dma("sp", qkg[b][:, ki, :], qkg_d[ki, hp * 128:(hp + 1) * 128, :], reads=[qkg_dep],
                      writes=[qkgs_dep[b]], chan="qkg%d" % b, accumulate=True)

        def load_slab(hp, pi, si):
            r = PATTERNS[pi][1]
            nblk = 32 // r
            sl = slab[si]
            vv = vscr_d.rearrange("(n p r) c -> r p n c", p=128, r=r)
            for res in range(r):
                for hh in range(2):
                    col = (2 * hp + hh) * 64
                    dcol = 0 if hh == 0 else 128
                    T.dma("sp", sl[:, res * nblk:(res + 1) * nblk, dcol:dcol + 64], vv[res, :, :, col:col + 64],
                          reads=[vscr_dep], writes=[slab_dep[si]], chan="slab%d" % si, accumulate=True)

        units = []
        for hp in range(4):
            for pi, (win, r) in enumerate(PATTERNS):
                nblk = 32 // r
                for res in range(r):
                    for n in range(nblk):
                        units.append((hp, pi, r, nblk, res, n))
        state = {}

        def stage12(u, k):
            hp, pi, r, nblk, res, n = u
            gi = hp * 3 + pi
            si = gi % 2
            li = res * nblk + n
            if li == DEPTH + 1:
                if gi + 1 < 12:
                    load_slab((gi + 1) // 3, (gi + 1) % 3, (gi + 1) % 2)
                if pi == 1 and hp + 1 < 4:
                    load_pair(hp + 1)
            qb = hp % 2
            qT = qkg[qb][:, 0, :]
            kT = qkg[qb][:, 1, :]
            qd = qkgs_dep[qb]
            t0 = res + r * 128 * n
            ncol = 256 if n + 1 < nblk else 128
            ksl = slice(t0, t0 + 127 * r + 1, r)
            qsl = slice(t0, t0 + (ncol - 1) * r + 1, r)
            ps, psd = PS_S()
            for hh in range(2):
                pb = hh * 64
                A("pe", lambda: nc.tensor.matmul(ps[:, hh * 512:hh * 512 + ncol], kT[pb:pb + 64, ksl], qT[pb:pb + 64, qsl],
                                                 start=True, stop=True), reads=[qd], writes=[psd])
            b = k % NB
            psv = ps[:, :].rearrange("p (h c) -> p h c", h=2)
            A("act", lambda: ACT(out=eT[b][:, :, 0:ncol], in_=psv[:, :, 0:ncol], func=AF.Exp), reads=[psd],
              writes=[eT_dep[b]])
            MP = int(os.environ.get("K_MULTPOOL", "2"))
            usep = True if MP == 0 else ((k % 3 != 0) if MP == 3 else (k % 2 == 0))
            eng_ = "pool" if usep else "dve"
            E_ = nc.gpsimd if usep else nc.vector
            if ncol == 256:
                fl = lambda t: t.rearrange("p h c -> p (h c)")
                A(eng_, lambda: E_.tensor_tensor(out=fl(pT[b][:, :, :]), in0=fl(eT[b][:, :, :]),
                                                 in1=fl(Et[:, pi * 8 + 2 * hp:pi * 8 + 2 * hp + 2, :]), op=ALU.mult),
                  reads=[eT_dep[b], Et_deps[pi * 8 + 2 * hp], Et_deps[pi * 8 + 2 * hp + 1]], writes=[pT_dep[b]])
            else:
                A(eng_, lambda: E_.tensor_tensor(out=pT[b][:, :, 0:ncol], in0=eT[b][:, :, 0:ncol],
                                                 in1=Et[:, pi * 8 + 2 * hp:pi * 8 + 2 * hp + 2, 0:ncol], op=ALU.mult),
                  reads=[eT_dep[b], Et_deps[pi * 8 + 2 * hp], Et_deps[pi * 8 + 2 * hp + 1]], writes=[pT_dep[b]])

        def stage3(u, k):
            hp, pi, r, nblk, res, n = u
            gi = hp * 3 + pi
            si = gi % 2
            sl = slab[si]
            ti = res * nblk + n
            t0 = res + r * 128 * n
            ncol = 256 if n + 1 < nblk else 128
            qsl = slice(t0, t0 + (ncol - 1) * r + 1, r)
            b = k % NB
            acc, acc_dep = accs[hp % 2], accs_dep[hp % 2]
            po, pod = PS_O()
            for hh in range(2):
                lc = 0 if hh == 0 else 64
                A("pe", lambda: nc.tensor.matmul(po[:, hh * 256:hh * 256 + ncol], sl[:, ti, lc:lc + 128],
                                                 pT[b][:, hh, 0:ncol], start=True, stop=True),
                  reads=[slab_dep[si], pT_dep[b]], writes=[pod])
            for hh in range(2):
                M = 65 if hh == 0 else 128
                A("dve", lambda: nc.vector.tensor_tensor(out=acc[hh][0:M, qsl], in0=po[0:M, hh * 256:hh * 256 + ncol],
                                                         in1=acc[hh][0:M, qsl], op=ALU.add),
                  reads=[pod, acc_dep[hh]], writes=[acc_dep[hh]])
            if pi == 2 and res == r - 1 and n == nblk - 1:
                fin_bg.append(finish_pair(hp))

        fin_bg = []

        def finish_pair(hp):
            qb = hp % 2
            gT = qkg[qb][:, 2, :]
            qd = qkgs_dep[qb]
            acc, acc_dep = accs[hp % 2], accs_dep[hp % 2]
            for hh, row in ((0, 64), (1, 0)):
                A("dve", lambda: nc.vector.tensor_copy(out=dhl[row:row + 1, 0, :], in_=acc[hh][row:row + 1, :]),
                  reads=[acc_dep[hh]], writes=[dhl_dep])
                yield
                A("dve", lambda: nc.vector.tensor_tensor(out=dhl[row:row + 1, 1, :], in0=acc[hh][row:row + 1, :],
                                                         in1=dhl[row:row + 1, 0, :], op=ALU.subtract),
                  reads=[acc_dep[hh], dhl_dep], writes=[dhl_dep])
                yield
            for tg in range(8):
                cs = slice(tg * 512, (tg + 1) * 512)
                yb = y_i[0] % 2
                y_i[0] += 1
                for hh in range(2):
                    pr = slice(0, 64) if hh == 0 else slice(64, 128)
                    ps, psd = PS_O()
                    for hl in range(2):
                        if hh == 0:
                            A("pe", lambda: nc.tensor.matmul(ps[0:64, :], cst["c_selA"][64:65, 0:64], dhl[64:65, hl, cs],
                                                             start=(hl == 0), stop=(hl == 1)),
                              reads=[dhl_dep, cdep], writes=[psd])
                        else:
                            A("pe", lambda: nc.tensor.matmul(ps[:, :], cst["c_selB"][0:1, :], dhl[0:1, hl, cs],
                                                             start=(hl == 0), stop=(hl == 1)),
                              reads=[dhl_dep, cdep], writes=[psd])
                    A("act", lambda: ACT(out=rden[hh][pr, :], in_=ps[pr, :], func=AF.Ln), reads=[psd],
                      writes=[rden_dep[hh]])
                    A("act", lambda: ACT(out=rden[hh][pr, :], in_=rden[hh][pr, :], func=AF.Exp, scale=-1.0),
                      reads=[rden_dep[hh]], writes=[rden_dep[hh]])
                    A("pool", lambda: nc.gpsimd.tensor_tensor(out=rden[hh][pr, :], in0=rden[hh][pr, :], in1=gT[pr, cs],
                                                              op=ALU.mult), reads=[rden_dep[hh], qd],
                      writes=[rden_dep[hh]])
                    A("dve", lambda: nc.vector.tensor_tensor(out=yst[yb][pr, :], in0=acc[hh][pr, cs], in1=rden[hh][pr, :],
                                                             op=ALU.mult), reads=[rden_dep[hh], acc_dep[hh]],
                      writes=[yst_dep[yb]])
                    yield
                T.dma("sp", mix_d[512 + hp * 128:512 + (hp + 1) * 128, cs], yst[yb][:], reads=[yst_dep[yb]],
                      writes=[mix_dep], chan="yst%d" % yb, accumulate=True)
            if hp + 2 < 4:
                for i in range(2):
                    A("pool", lambda: nc.gpsimd.memset(acc[i][:], 0.0), writes=[acc_dep[i]])
            yield

        load_pair(0)
        load_slab(0, 0, 0)
        n_et = 0
        for hp_ in range(4):
            for pi_ in range(3):
                for hh_ in range(2):
                    idx = pi_ * 8 + 2 * hp_ + hh_
                    src_ap = bass.AP(tensor=gscr_d.tensor, offset=idx * 128 * 384 + 127,
                                     ap=[[383, 128], [128, 2], [1, 128]])
                    T.dma("sp" if n_et % 2 == 0 else "act", Et[:, idx, :].rearrange("p (a b) -> p a b", a=2), src_ap,
                          reads=[g_dep], writes=[Et_deps[idx]], chan="et%d" % idx)
                    n_et += 1
        nU = min(len(units), int(os.environ.get('K_NU', '100000')))
        FILL_ATT = int(os.environ.get("K_FILL_ATT", "0"))
        for i in range(nU + DEPTH):
            if i >= DEPTH:
                stage3(units[i - DEPTH], i - DEPTH)
            E["filler"](FILL_ATT)
            if i < nU:
                stage12(units[i], i)
            if i == 40:
                E["issue_wo_loads"]()
            nxt = []
            for g_ in fin_bg:
                try:
                    next(g_)
                    nxt.append(g_)
                except StopIteration:
                    pass
            fin_bg[:] = nxt
        while fin_bg:
            nxt = []
            for g_ in fin_bg:
                try:
                    next(g_)
                    nxt.append(g_)
                except StopIteration:
                    pass
            fin_bg[:] = nxt
    T.barrier()


def phase_dn(nc, T, E):
    sb, A, ACT, PSF = E["sb"], E["A"], E["ACT"], E["PSF"]
    PSB = E["PSB"]
    filler = E["filler"]
    FILL_DN = int(os.environ.get("K_FILL_DN", "0"))
    hT, hT_all, cst, cdep = E["hT"], E["hT_dep"], E["cst"], E["cdep"]
    mix_d, mix_dep = E["mix_d"], E["mix_dep"]
    convw, alog_bc, dtb_bc, dnw = E["convw"], E["alog_bc"], E["dtb_bc"], E["dnw"]
    ident_f, ident_b = cst["c_ident_f"], cst["c_ident_b"]
    ones_f, nones_f, ones_b, tri = cst["c_ones_f"], cst["c_nones_f"], cst["c_ones_b"], cst["c_tri"]
    mu_b, ml_b = cst["c_mu"], cst["c_ml"]
    V = nc.vector
    G = nc.gpsimd

    def mm(out, lhsT, rhs, start, stop, reads, writes):
        A("pe", lambda: nc.tensor.matmul(out, lhsT, rhs, start=start, stop=stop), reads=reads, writes=writes)

    with ExitStack() as pd:
        wdn, wdn_dep = E["wdn"], E["wdn_dep"]
        diagc = sb("diagc", [128, 12, 4, 128], BF16, pd)
        diagc_dep = Dep()
        for t in range(12):
            for i in range(4):
                A("pool", lambda: G.tensor_scalar(out=diagc[:, t, i, :], in0=ident_f[:], scalar1=convw[:, t, i:i + 1],
                                                  scalar2=0.0, op0=ALU.mult, op1=ALU.add),
                  reads=[cdep], writes=[diagc_dep])
        raw = sb("raw", [128, 12, 515], BF16, pd)
        raw_dep = [Dep() for _ in range(12)]
        for t in range(12):
            A("pool", lambda: G.memset(raw[:, t, 0:3], 0.0), writes=[raw_dep[t]])
        Sb = sb("Sb", [128, 4, 128], BF16, pd)
        S_dep = Dep("S")
        A("pool", lambda: G.memset(Sb[:], 0.0), writes=[S_dep])
        dtb16 = sb("dtb16", [128, 4, 4], F32, pd)
        nA16 = sb("nA16", [128, 4, 4], F32, pd)
        c16_dep = Dep()
        nA = sb("nA", [128, 4], F32, pd)
        A("act", lambda: ACT(out=nA[:], in_=alog_bc[:], func=AF.Exp), reads=[cdep], writes=[c16_dep])
        for b in range(4):
            A("dve", lambda: V.tensor_scalar(out=nA16[:, b, :], in0=nA[:], scalar1=-1.0, scalar2=None, op0=ALU.mult),
              reads=[c16_dep], writes=[c16_dep])
            A("dve", lambda: V.tensor_copy(out=dtb16[:, b, :], in_=dtb_bc[:]), reads=[cdep], writes=[c16_dep])

        def t4(name, ty=BF16, w=128):
            return sb(name, [128, 4, w], ty, pd), Dep(name)

        GT = []
        for i in range(2):
            q_, qd_ = t4("qT4_%d" % i, BF16, 512)
            k_, kd_ = t4("kT4_%d" % i, BF16, 512)
            v_, vd_ = t4("vT4_%d" % i, BF16, 512)
            GT.append((q_, qd_, k_, kd_, v_, vd_))
        zsb = sb("zsb", [128, 512], BF16, pd)
        zsb_d = Dep()
        oT4, oT4_d = t4("oT4", F32, 512)
        sqb = sb("sqb", [128, 512], BF16, pd)
        sqb_d = Dep()
        lnb = sb("lnb", [128, 512], F32, pd)
        lnb_d = Dep()
        sqb2, sqb2_d, lnb2, lnb2_d = sqb, sqb_d, lnb, lnb_d
        rinv = sb("rinv", [128, 512], BF16, pd)
        rinv_d = Dep()
        rinvf = sb("rinvf", [128, 512], F32, pd)
        rinvf_d = Dep()
        yst = [sb("ydn%d" % i, [128, 512], BF16, pd) for i in range(2)]
        yst_d = [Dep() for _ in range(2)]
        ba = sb("ba", [128, 4, 8], F32, pd)
        ba_d = Dep()
        SM = []
        for i in range(2):
            sm_ = {n: sb("sm%d_" % i + n, [128, 4, 4], F32, pd) for n in
                   ("beta", "nbeta", "x1", "ax", "e", "l", "mx", "sp", "g", "gc", "ngc", "egc", "ebg", "dlt", "etail",
                    "gl")}
            SM.append((sm_, {n: Dep("sm_" + n) for n in sm_}))
        diag4, diag4_d = t4("diag4", F32)
        rgmu, rgmu_d = t4("rgmu", F32)
        rgml, rgml_d = t4("rgml", F32)
        mu4, mk4_d = t4("mu4", BF16)
        ml4, _ = t4("ml4", BF16)
        for h in range(4):
            A("pool", lambda: G.tensor_copy(out=mu4[:, h, :], in_=mu_b[:]), reads=[cdep], writes=[mk4_d])
            A("pool", lambda: G.tensor_copy(out=ml4[:, h, :], in_=ml_b[:]), reads=[cdep], writes=[mk4_d])
        DT4, DT4_d = t4("DT4")
        D4, D4_d = t4("D4")
        Eg4, Eg4_d = t4("Eg4")
        UW4, UW4_d = t4("UW4", BF16, 256)
        Mn4, Mn4_d = t4("Mn4")
        Qe4, Qe4_d = t4("Qe4")
        yi = [0]
        LNQ = -0.5 * math.log(128.0)
        f4 = lambda t: t[:].rearrange("p h w -> p (h w)")
        PRE = []
        for i in range(2):
            P = {}
            for n_, w_ in (("N4", 128), ("Y4", 128), ("ka4", 256), ("qd4", 128), ("vbk4", 256)):
                P[n_], P[n_ + "d"] = t4("%s_%d" % (n_, i), BF16, w_)
            PRE.append(P)
        XYR = [[(sb("xyr%d_%d" % (h, i), [128, 384], BF16, pd), Dep()) for i in range(2)] for h in range(4)]
        R6t = [(sb("r6_%d" % h, [128, 128], BF16, pd), Dep()) for h in range(4)]
        UWd = [Dep() for _ in range(4)]
        Mnd = [Dep() for _ in range(4)]
        Qed = [Dep() for _ in range(4)]
        psf, psf_dep = E["psf"], E["psf_dep"]
        rot2 = [0]

        pf2 = [(psf[4], psf_dep[4]), (psf[5], psf_dep[5]), (E["psb"][1][:, :].bitcast(F32), E["psb_dep"][1])]

        def PSF2():
            i = rot2[0]
            rot2[0] = (i + 1) % 3
            return pf2[i]

        PSB = lambda: (E["psb"][0], E["psb_dep"][0])
        if os.environ.get("K_FRONT_PSF2", "1") == "1":
            PSF = PSF2

        def group_fns(tg):
            ts_ = slice(tg * 512, (tg + 1) * 512)
            hdeps = hT_all[tg * 4:tg * 4 + 4]
            GTp = GT[tg % 2]
            qT4, qT4_d, kT4, kT4_d, vT4, vT4_d = GTp
            qs4, qs4_d, ks4, ks4_d = qT4, qT4_d, kT4, kT4_d
            sm, sm_d = SM[tg % 2]
            def gen_front():
                for ct in range(12):
                    ps, psd = PSF()
                    for c in range(8):
                        mm(ps[:, :], wdn[:, c, ct * 128:(ct + 1) * 128], hT[:, c, ts_], c == 0, c == 7,
                           hdeps + [wdn_dep], [psd])
                    if tg > 0:
                        A("pool", lambda: G.tensor_copy(out=raw[:, ct, 0:3], in_=raw[:, ct, 512:515]), reads=[raw_dep[ct]],
                          writes=[raw_dep[ct]])
                    A("dve", lambda: V.tensor_copy(out=raw[:, ct, 3:515], in_=ps[:, :]), reads=[psd], writes=[raw_dep[ct]])
                    ps2, ps2d = PSF()
                    for i in range(4):
                        mm(ps2[:, :], diagc[:, ct, i, :], raw[:, ct, i:i + 512], i == 0, i == 3, [diagc_dep, raw_dep[ct]],
                           [ps2d])
                    dst, dd = ((qs4, qs4_d), (ks4, ks4_d), (vT4, vT4_d))[ct // 4]
                    A("act", lambda: ACT(out=dst[:, ct % 4, :], in_=ps2[:, :], func=AF.Silu), reads=[ps2d], writes=[dd])
                    yield
                for which in range(2):
                    src, sd = (qs4, qs4_d) if which == 0 else (ks4, ks4_d)
                    dst, dd = (qT4, qT4_d) if which == 0 else (kT4, kT4_d)
                    for h in range(4):
                        A("act", lambda: ACT(out=sqb[:], in_=src[:, h, :], func=AF.Square), reads=[sd], writes=[sqb_d])
                        ps, psd = PSF()
                        mm(ps[:, :], ones_b[:], sqb[:], True, True, [sqb_d, cdep], [psd])
                        A("act", lambda: ACT(out=lnb[:], in_=ps[:, :], func=AF.Ln, bias=EPS), reads=[psd], writes=[lnb_d])
                        A("act", lambda: ACT(out=rinv[:], in_=lnb[:], func=AF.Exp, scale=-0.5,
                                             bias=(LNQ if which == 0 else 0.0)), reads=[lnb_d], writes=[rinv_d])
                        A("dve", lambda: V.tensor_tensor(out=dst[:, h, :], in0=src[:, h, :], in1=rinv[:], op=ALU.mult),
                          reads=[sd, rinv_d], writes=[dd])
                        yield
                ps, psd = PSF()
                for b in range(4):
                    tt = tg * 4 + b
                    for c in range(8):
                        mm(ps[:, b * 8:(b + 1) * 8], hT[:, c, tt * 128:(tt + 1) * 128], wdn[:, c, 2048:2056], c == 0, c == 7,
                           [hT_all[tt], wdn_dep], [psd])
                A("dve", lambda: V.tensor_copy(out=ba[:], in_=ps[:, 0:32].rearrange("p (b e) -> p b e", b=4)), reads=[psd],
                  writes=[ba_d])
                A("act", lambda: ACT(out=sm["beta"][:], in_=ba[:, :, 0:4], func=AF.Sigmoid), reads=[ba_d],
                  writes=[sm_d["beta"]])
                A("dve", lambda: V.tensor_scalar(out=sm["nbeta"][:], in0=sm["beta"][:], scalar1=-1.0, scalar2=None,
                                                 op0=ALU.mult), reads=[sm_d["beta"]], writes=[sm_d["nbeta"]])
                A("dve", lambda: V.tensor_tensor(out=sm["x1"][:], in0=ba[:, :, 4:8], in1=dtb16[:], op=ALU.add),
                  reads=[ba_d, c16_dep], writes=[sm_d["x1"]])
                A("dve", lambda: V.tensor_scalar(out=sm["mx"][:], in0=sm["x1"][:], scalar1=-1.0, scalar2=None, op0=ALU.mult),
                  reads=[sm_d["x1"]], writes=[sm_d["mx"]])
                A("dve", lambda: V.tensor_tensor(out=sm["ax"][:], in0=sm["x1"][:], in1=sm["mx"][:], op=ALU.max),
                  reads=[sm_d["x1"], sm_d["mx"]], writes=[sm_d["ax"]])
                A("act", lambda: ACT(out=sm["e"][:], in_=sm["ax"][:], func=AF.Exp, scale=-1.0), reads=[sm_d["ax"]],
                  writes=[sm_d["e"]])
                A("act", lambda: ACT(out=sm["l"][:], in_=sm["e"][:], func=AF.Ln, bias=1.0), reads=[sm_d["e"]],
                  writes=[sm_d["l"]])
                A("dve", lambda: V.tensor_scalar(out=sm["mx"][:], in0=sm["x1"][:], scalar1=0.0, scalar2=None, op0=ALU.max),
                  reads=[sm_d["x1"]], writes=[sm_d["mx"]])
                A("dve", lambda: V.tensor_tensor(out=sm["sp"][:], in0=sm["mx"][:], in1=sm["l"][:], op=ALU.add),
                  reads=[sm_d["mx"], sm_d["l"]], writes=[sm_d["sp"]])
                A("dve", lambda: V.tensor_tensor(out=sm["g"][:], in0=sm["sp"][:], in1=nA16[:], op=ALU.mult),
                  reads=[sm_d["sp"], c16_dep], writes=[sm_d["g"]])
                ps, psd = PSF()
                g16 = sm["g"][:].rearrange("p b h -> p (b h)")
                mm(ps[:, 0:16], tri[:], g16, True, True, [sm_d["g"], cdep], [psd])
                mm(ps[:, 16:32], ones_f[:], g16, True, True, [sm_d["g"], cdep], [psd])
                v16 = lambda n: sm[n][:].rearrange("p b h -> p (b h)")
                A("dve", lambda: V.tensor_copy(out=v16("gc"), in_=ps[:, 0:16]), reads=[psd], writes=[sm_d["gc"]])
                A("act", lambda: ACT(out=v16("egc"), in_=ps[:, 0:16], func=AF.Exp), reads=[psd], writes=[sm_d["egc"]])
                A("dve", lambda: V.tensor_scalar(out=v16("ngc"), in0=v16("gc"), scalar1=-1.0, scalar2=None, op0=ALU.mult),
                  reads=[sm_d["gc"]], writes=[sm_d["ngc"]])
                A("dve", lambda: V.tensor_tensor(out=v16("ebg"), in0=v16("egc"), in1=v16("beta"), op=ALU.mult),
                  reads=[sm_d["egc"], sm_d["beta"]], writes=[sm_d["ebg"]])
                A("dve", lambda: V.tensor_tensor(out=v16("dlt"), in0=ps[:, 16:32], in1=v16("gc"), op=ALU.subtract),
                  reads=[psd, sm_d["gc"]], writes=[sm_d["dlt"]])
                A("act", lambda: ACT(out=v16("etail"), in_=v16("dlt"), func=AF.Exp), reads=[sm_d["dlt"]],
                  writes=[sm_d["etail"]])
                A("act", lambda: ACT(out=v16("gl"), in_=ps[:, 16:32], func=AF.Exp), reads=[psd], writes=[sm_d["gl"]])

                yield

            def gen_pre(b):
                P = PRE[b % 2]
                bs = slice(b * 128, (b + 1) * 128)
                sc = lambda n, h: sm[n][:, b, h:h + 1]
                for h in range(4):
                    A("pool", lambda: G.tensor_scalar(out=diag4[:, h, :], in0=ones_f[:], scalar1=sc("gc", h),
                                                      scalar2=0.0, op0=ALU.mult, op1=ALU.add),
                      reads=[cdep, sm_d["gc"]], writes=[diag4_d])
                pR, pRd = PSF2()
                for h in range(4):
                    hs = slice(h * 128, (h + 1) * 128)
                    A("pe", lambda: nc.tensor.transpose(pR[:, hs], diag4[:, h, :], ident_f[:]), reads=[diag4_d, cdep],
                      writes=[pRd])
                A("act", lambda: ACT(out=f4(Eg4), in_=pR[:, :], func=AF.Exp), reads=[pRd], writes=[Eg4_d])
                A("dve", lambda: V.tensor_tensor(out=f4(rgmu), in0=pR[:, :], in1=f4(mu4), op=ALU.add),
                  reads=[pRd, mk4_d], writes=[rgmu_d])
                A("dve", lambda: V.scalar_tensor_tensor(out=f4(rgml), in0=pR[:, :], scalar=-1.0, in1=f4(ml4),
                                                        op0=ALU.mult, op1=ALU.add),
                  reads=[pRd, mk4_d], writes=[rgml_d])
                for h in range(4):
                    A("act", lambda: ACT(out=DT4[:, h, :], in_=rgmu[:, h, :], func=AF.Exp, bias=sc("ngc", h)),
                      reads=[rgmu_d, sm_d["ngc"]], writes=[DT4_d])
                    A("act", lambda: ACT(out=D4[:, h, :], in_=rgml[:, h, :], func=AF.Exp, bias=sc("gc", h)),
                      reads=[rgml_d, sm_d["gc"]], writes=[D4_d])
                yield
                pG, pGd = PSF2()
                for h in range(4):
                    hs = slice(h * 128, (h + 1) * 128)
                    mm(pG[:, hs], kT4[:, h, bs], kT4[:, h, bs], True, True, [kT4_d], [pGd])
                bcs = lambda n_: sm[n_][:, b, :].unsqueeze(2).to_broadcast([128, 4, 128])
                A("pool", lambda: G.tensor_tensor(out=D4[:], in0=D4[:], in1=bcs("nbeta"), op=ALU.mult),
                  reads=[D4_d, sm_d["nbeta"]], writes=[D4_d])
                A("dve", lambda: V.tensor_tensor(out=f4(P["N4"]), in0=pG[:, :], in1=f4(D4), op=ALU.mult),
                  reads=[pGd, D4_d], writes=[P["N4d"]])
                yield
                pQ, pQd = PSF2()
                for h in range(4):
                    hs = slice(h * 128, (h + 1) * 128)
                    mm(pQ[:, hs], kT4[:, h, bs], qT4[:, h, bs], True, True, [kT4_d, qT4_d], [pQd])
                A("dve", lambda: V.tensor_tensor(out=P["ka4"][:, :, 128:256],
                                                 in0=pQ[:, :].rearrange("p (h w) -> p h w", h=4), in1=DT4[:],
                                                 op=ALU.mult), reads=[pQd, DT4_d], writes=[P["ka4d"]])
                A("pool", lambda: G.tensor_tensor(out=P["qd4"][:], in0=qT4[:, :, bs], in1=Eg4[:], op=ALU.mult),
                  reads=[qT4_d, Eg4_d], writes=[P["qd4d"]])
                yield
                pT1, pT1d = PSB()
                for h in range(4):
                    hs = slice(h * 128, (h + 1) * 128)
                    A("pe", lambda: nc.tensor.transpose(pT1[:, hs], P["N4"][:, h, :], ident_b[:]),
                      reads=[P["N4d"], cdep], writes=[pT1d])
                A("act", lambda: nc.scalar.copy(out=f4(P["Y4"]), in_=pT1[:, 0:512]), reads=[pT1d], writes=[P["Y4d"]])
                yield
                pT2, pT2d = PSB()
                for h in range(4):
                    hs = slice(h * 128, (h + 1) * 128)
                    A("pe", lambda: nc.tensor.transpose(pT2[:, hs], vT4[:, h, bs], ident_b[:]), reads=[vT4_d, cdep],
                      writes=[pT2d])
                    A("pe", lambda: nc.tensor.transpose(pT2[:, 512 + h * 128:512 + (h + 1) * 128], kT4[:, h, bs],
                                                        ident_b[:]), reads=[kT4_d, cdep], writes=[pT2d])
                v4 = pT2[:, 0:512].rearrange("p (h w) -> p h w", h=4)
                k4 = pT2[:, 512:1024].rearrange("p (h w) -> p h w", h=4)
                A("dve", lambda: V.tensor_tensor(out=P["vbk4"][:, :, 0:128], in0=v4, in1=bcs("beta"), op=ALU.mult),
                  reads=[pT2d, sm_d["beta"]], writes=[P["vbk4d"]])
                A("dve", lambda: V.tensor_tensor(out=P["vbk4"][:, :, 128:256], in0=k4, in1=bcs("ebg"), op=ALU.mult),
                  reads=[pT2d, sm_d["ebg"]], writes=[P["vbk4d"]])
                A("dve", lambda: V.tensor_tensor(out=P["ka4"][:, :, 0:128], in0=k4, in1=bcs("etail"), op=ALU.mult),
                  reads=[pT2d, sm_d["etail"]], writes=[P["ka4d"]])
                yield

            def gen_inv(b, h):
                P = PRE[b % 2]
                bs = slice(b * 128, (b + 1) * 128)
                sc = lambda n: sm[n][:, b, h:h + 1]
                bank, bankd = psf[h], psf_dep[h]
                X, Xd = P["N4"][:, h, :], P["N4d"]
                Y, Yd = P["Y4"][:, h, :], P["Y4d"]
                YR = None
                R = None
                for j in range(1, 7):
                    nt, ntd = XYR[h][j % 2]
                    ev = "act" if ((h + j) % 2 == 0 and os.environ.get("K_EVDVE", "0") != "1") else "dve"
                    mm(bank[:, 0:128], Y, X, True, True, [Xd, Yd], [bankd])
                    if j == 1:
                        mm(bank[:, 128:256], X, Y, True, True, [Xd, Yd], [bankd])
                        A("pool", lambda: G.tensor_tensor(out=nt[:, 256:384], in0=ident_b[:], in1=Y, op=ALU.add),
                          reads=[cdep, Yd], writes=[ntd])
                        w = 256
                    elif j < 6:
                        mm(bank[:, 128:384], X, YR, True, False, [Xd], [bankd])
                        mm(bank[:, 256:384], ident_b[:], R, False, True, [cdep, Xd], [bankd])
                        w = 384
                    else:
                        mm(bank[:, 128:256], X, R, True, False, [Xd], [bankd])
                        mm(bank[:, 128:256], ident_b[:], R, False, True, [cdep, Xd], [bankd])
                        w = 256
                    if ev == "act":
                        A("act", lambda: nc.scalar.copy(out=nt[:, 0:w], in_=bank[:, 0:w]), reads=[bankd], writes=[ntd])
                    else:
                        A("dve", lambda: V.tensor_copy(out=nt[:, 0:w], in_=bank[:, 0:w]), reads=[bankd], writes=[ntd])
                    X, Xd = nt[:, 0:128], ntd
                    if j < 6:
                        Y, Yd = nt[:, 128:256], ntd
                        YR = nt[:, 128:384]
                        R = nt[:, 256:384]
                    else:
                        R = nt[:, 128:256]
                    yield
                mm(bank[:, 0:128], X, R, True, False, [Xd], [bankd])
                mm(bank[:, 0:128], ident_b[:], R, False, True, [cdep, Xd], [bankd])
                r6, r6d = R6t[h]
                A("dve" if h % 2 == 0 else "act",
                  (lambda: V.tensor_copy(out=r6[:], in_=bank[:, 0:128])) if h % 2 == 0 else
                  (lambda: nc.scalar.copy(out=r6[:], in_=bank[:, 0:128])), reads=[bankd], writes=[r6d])
                yield
                mm(bank[:, 0:256], r6[:], P["vbk4"][:, h, :], True, True, [r6d, P["vbk4d"]], [bankd])
                A("act" if h % 2 == 0 else "dve",
                  (lambda: nc.scalar.copy(out=UW4[:, h, :], in_=bank[:, 0:256])) if h % 2 == 0 else
                  (lambda: V.tensor_copy(out=UW4[:, h, :], in_=bank[:, 0:256])), reads=[bankd], writes=[UWd[h]])
                yield
                mm(bank[:, 0:256], UW4[:, h, 128:256], P["ka4"][:, h, :], True, True, [UWd[h], P["ka4d"]], [bankd])
                A("dve", lambda: V.scalar_tensor_tensor(out=Mn4[:, h, :], in0=ident_f[:], scalar=sc("gl"),
                                                        in1=bank[:, 0:128], op0=ALU.mult, op1=ALU.subtract),
                  reads=[bankd, cdep, sm_d["gl"]], writes=[Mnd[h]])
                A("dve", lambda: V.tensor_tensor(out=Qe4[:, h, :], in0=P["qd4"][:, h, :], in1=bank[:, 128:256],
                                                 op=ALU.subtract), reads=[P["qd4d"], bankd], writes=[Qed[h]])
                yield

            def emit_seq(b):
                P = PRE[b % 2]
                bs = slice(b * 128, (b + 1) * 128)
                pO, pOd = PSF2()
                pS, pSd = PSF2()
                for h in range(4):
                    hs = slice(h * 128, (h + 1) * 128)
                    mm(pO[:, hs], Sb[:, h, :], Qe4[:, h, :], True, False, [S_dep, Qed[h]], [pOd])
                    mm(pO[:, hs], UW4[:, h, 0:128], P["ka4"][:, h, 128:256], False, True, [UWd[h], P["ka4d"]], [pOd])
                for h in range(4):
                    hs = slice(h * 128, (h + 1) * 128)
                    mm(pS[:, hs], Mn4[:, h, :], Sb[:, h, :], True, False, [Mnd[h], S_dep], [pSd])
                    mm(pS[:, hs], P["ka4"][:, h, 0:128], UW4[:, h, 0:128], False, True, [P["ka4d"], UWd[h]], [pSd])
                A("act", lambda: nc.scalar.copy(out=oT4[:, :, bs], in_=pO[:, :].rearrange("p (h w) -> p h w", h=4)),
                  reads=[pOd], writes=[oT4_d])
                A("dve", lambda: V.tensor_copy(out=f4(Sb), in_=pS[:, :]), reads=[pSd], writes=[S_dep])

            def gen_g5():
                for h in range(4):
                    ps, psd = PSF2()
                    for c in range(8):
                        mm(ps[:, :], wdn[:, c, (12 + h) * 128:(13 + h) * 128], hT[:, c, ts_], c == 0, c == 7,
                           hdeps + [wdn_dep], [psd])
                    A("act", lambda: ACT(out=zsb[:], in_=ps[:, :], func=AF.Silu), reads=[psd], writes=[zsb_d])
                    A("act", lambda: ACT(out=sqb2[:], in_=oT4[:, h, :], func=AF.Square), reads=[oT4_d], writes=[sqb2_d])
                    ps, psd = PSF2()
                    mm(ps[:, :], ones_b[:], sqb2[:], True, True, [sqb2_d, cdep], [psd])
                    A("act", lambda: ACT(out=lnb2[:], in_=ps[:, :], func=AF.Ln, bias=EPS, scale=1.0 / 128), reads=[psd],
                      writes=[lnb2_d])
                    A("act", lambda: ACT(out=rinvf[:], in_=lnb2[:], func=AF.Exp, scale=-0.5), reads=[lnb2_d],
                      writes=[rinvf_d])
                    yield
                    A("pool", lambda: G.tensor_tensor(out=rinvf[:], in0=rinvf[:], in1=zsb[:], op=ALU.mult),
                      reads=[rinvf_d, zsb_d], writes=[rinvf_d])
                    yb = yi[0] % 2
                    yi[0] += 1
                    A("dve", lambda: V.scalar_tensor_tensor(out=yst[yb][:], in0=oT4[:, h, :], scalar=dnw[:, 0:1],
                                                            in1=rinvf[:], op0=ALU.mult, op1=ALU.mult),
                      reads=[oT4_d, rinvf_d, cdep], writes=[yst_d[yb]])
                    T.dma("sp", mix_d[h * 128:(h + 1) * 128, ts_], yst[yb][:], reads=[yst_d[yb]], writes=[mix_dep],
                          chan="ydn%d" % yb, accumulate=True)
                    yield

            return gen_front, gen_pre, gen_inv, emit_seq, gen_g5

        def step_all(gens):
            nxt = []
            for g_ in gens:
                try:
                    next(g_)
                    nxt.append(g_)
                    filler(FILL_DN)
                except StopIteration:
                    pass
            return nxt

        def drain(gens):
            while gens:
                gens = step_all(gens)

        fns = [group_fns(tg) for tg in range(8)]
        drain([fns[0][0]()])
        drain([fns[0][1](0)])
        g5_bg = []
        for tg in range(8):
            gen_front, gen_pre, gen_inv, emit_seq, gen_g5 = fns[tg]
            fr_bg = [fns[tg + 1][0]()] if tg + 1 < 8 else []
            if os.environ.get("K_NOBG", "0") == "1":
                drain(g5_bg)
                g5_bg = []
            for b in range(4):
                gl_ = [gen_inv(b, h) for h in range(4)]
                if b + 1 < 4:
                    gl_.append(gen_pre(b + 1))
                while gl_:
                    gl_ = step_all(gl_)
                    if os.environ.get("K_NOBG", "0") != "1":
                        if os.environ.get("K_NOFR", "0") != "1":
                            fr_bg = step_all(fr_bg)
                        if os.environ.get("K_NOG5", "0") != "1":
                            g5_bg = step_all(g5_bg)
                if b == 0:
                    drain(g5_bg)
                    g5_bg = []
                emit_seq(b)
            drain(fr_bg)
            g5_bg = [gen_g5()]
            if tg + 1 < 8:
                drain([fns[tg + 1][1](0)])
        drain(g5_bg)
        T.barrier()


_CACHE = {}


def kernel(**inputs):
    if "nc" not in _CACHE:
        _CACHE["nc"] = build()
    nc = _CACHE["nc"]
    consts = host_constants()
    x = np.ascontiguousarray(np.asarray(inputs["x"], dtype=np.float32))
    shared = {
        "norm_w": np.ascontiguousarray(np.asarray(inputs["norm_w"], np.float32)[0]),
        "w_in": np.ascontiguousarray(np.asarray(inputs["w_in"], np.float32)[0]),
        "conv_w": np.ascontiguousarray(np.asarray(inputs["conv_w"], np.float32)[0]),
        "a_log": np.ascontiguousarray(np.asarray(inputs["a_log"], np.float32)[0]),
        "dt_bias": np.ascontiguousarray(np.asarray(inputs["dt_bias"], np.float32)[0]),
        "dn_norm_w": np.ascontiguousarray(np.asarray(inputs["dn_norm_w"], np.float32)[0]),
        "q_norm_w": np.ascontiguousarray(np.asarray(inputs["q_norm_w"], np.float32)[0]),
        "k_norm_w": np.ascontiguousarray(np.asarray(inputs["k_norm_w"], np.float32)[0]),
        "rel_bias": np.ascontiguousarray(np.asarray(inputs["rel_bias"], np.float32)),
        "w_out": np.ascontiguousarray(np.asarray(inputs["w_out"], np.float32)[0]),
    }
    shared.update(consts)
    in_maps = []
    for b in range(8):
        m = dict(shared)
        m["x"] = x[b]
        in_maps.append(m)
    res = run_bass_kernel_spmd(nc, in_maps, core_ids=list(range(8)))
    return np.stack([np.asarray(r["out"], dtype=np.float32).reshape(S, D) for r in res.results], axis=0)
```
